# Optimizing a Trainium2 kernel written in Bass

```python
import math
import jax
import jax.numpy as jnp
from jax import lax
import numpy as np

D_MODEL = 2048
BATCH = 4
SEQ = 2048
DEPTH = 2

F32 = jnp.float32
EPS = 1e-6
MIX_WIDTH = D_MODEL
HALF = MIX_WIDTH // 2
N_EVEN = (DEPTH + 1) // 2
N_ODD = DEPTH // 2

GDN_HEADS = 8
GDN_DK = HALF // GDN_HEADS
GDN_DV = HALF // GDN_HEADS
GDN_CONV = 4
GDN_CHUNK = 64
S5_GROUP = 16
S5_GROUPS = HALF // S5_GROUP
S5_STATE = 64
RWKV_HEAD = 64
RWKV_HEADS = HALF // RWKV_HEAD
RWKV_DECAY_LORA = 64
RWKV_ICLR_LORA = 64
RWKV_GATE_LORA = 160
RWKV_GN_EPS = 64e-5
RET_HEADS = 4
RET_DK = HALF // 2 // RET_HEADS
RET_DV = HALF // RET_HEADS
RET_CHUNK = 128
ROPE_BASE = 10000.0
D_FF = 256 * ((8 * D_MODEL // 3 + 255) // 256)
FFN_CONV = 3

EVEN_COLS = (HALF, HALF, HALF, HALF, GDN_HEADS, GDN_HEADS, HALF)
RWKV_COLS = (HALF, HALF, HALF, RWKV_DECAY_LORA, RWKV_ICLR_LORA, RWKV_GATE_LORA)
RWKV_WIDTH = sum(RWKV_COLS)
RET_COLS = (RET_HEADS * RET_DK, RET_HEADS * RET_DK, RET_HEADS * RET_DV, HALF)
EVEN_PROJ = sum(EVEN_COLS)
ODD_PROJ = RWKV_WIDTH + sum(RET_COLS)

kernel_name = 'hybrid_gdn_s5_rwkv7_retention_trunk'


def rms_norm(x, w, eps=EPS):
    xf = x.astype(F32)
    xf = xf * lax.rsqrt(jnp.mean(xf * xf, axis=-1, keepdims=True) + eps)
    return xf * w.astype(F32)


def l2norm(t, eps=1e-6):
    return t * lax.rsqrt(jnp.sum(t * t, axis=-1, keepdims=True) + eps)


def split_cols(z, sizes):
    idx = [int(i) for i in np.cumsum(sizes)[:-1]]
    return jnp.split(z, idx, axis=-1)


def causal_dwconv(x, w):
    k = w.shape[0]
    return lax.conv_general_dilated(
        x, w.astype(x.dtype)[:, None, :], window_strides=(1,), padding=[(k - 1, 0)],
        dimension_numbers=('NWC', 'WIO', 'NWC'), feature_group_count=x.shape[-1])


def token_shift(z):
    return jnp.pad(z, ((0, 0), (1, 0), (0, 0)))[:, :-1]


def to_chunks(t, c):
    b, T, h = t.shape[:3]
    return jnp.moveaxis(t.reshape(b, T // c, c, h, *t.shape[3:]), 2, 3)


def from_chunks(t):
    b, n, h, c = t.shape[:4]
    return jnp.moveaxis(t, 3, 2).reshape(b, n * c, h, *t.shape[4:])


def gated_deltanet(q, k, v, gate, beta_raw, alpha_raw, conv_w, a_log, dt_bias, norm_w):
    b, T, _ = q.shape
    qkv = jax.nn.silu(causal_dwconv(jnp.concatenate([q, k, v], axis=-1).astype(F32), conv_w))
    q, k, v = jnp.split(qkv, 3, axis=-1)
    q = l2norm(q.reshape(b, T, GDN_HEADS, GDN_DK)) * (GDN_DK ** -0.5)
    k = l2norm(k.reshape(b, T, GDN_HEADS, GDN_DK))
    v = v.reshape(b, T, GDN_HEADS, GDN_DV)
    beta = jax.nn.sigmoid(beta_raw.astype(F32))
    g = -jnp.exp(a_log.astype(F32)) * jax.nn.softplus(alpha_raw.astype(F32) + dt_bias.astype(F32))
    c = GDN_CHUNK
    qc, kc, vc = to_chunks(q, c), to_chunks(k, c), to_chunks(v, c)
    gc = jnp.cumsum(to_chunks(g, c), axis=-1)
    bc = to_chunks(beta, c)
    causal = jnp.tril(jnp.ones((c, c), dtype=bool))
    strict = jnp.tril(jnp.ones((c, c), dtype=bool), -1)
    decay = jnp.exp(jnp.where(causal, gc[..., :, None] - gc[..., None, :], -jnp.inf))
    kb = kc * bc[..., None]
    lmat = jnp.where(strict, jnp.einsum('bnhid,bnhjd->bnhij', kb, kc) * decay, 0.0) + jnp.eye(c, dtype=F32)
    rhs = jnp.concatenate([vc * bc[..., None], kb * jnp.exp(gc)[..., None]], axis=-1)
    sol = lax.linalg.triangular_solve(lmat, rhs, left_side=True, lower=True, unit_diagonal=True)
    u, w = sol[..., :GDN_DV], sol[..., GDN_DV:]
    attn = jnp.einsum('bnhid,bnhjd->bnhij', qc, kc) * decay
    qg = qc * jnp.exp(gc)[..., None]
    g_last = gc[..., -1]
    kdec = kc * jnp.exp(g_last[..., None] - gc)[..., None]

    def step(S, xs):
        qg_n, kdec_n, u_n, w_n, attn_n, gl_n = xs
        v_new = u_n - jnp.einsum('bhcd,bhde->bhce', w_n, S)
        o = jnp.einsum('bhcd,bhde->bhce', qg_n, S) + jnp.einsum('bhij,bhje->bhie', attn_n, v_new)
        S = S * jnp.exp(gl_n)[..., None, None] + jnp.einsum('bhcd,bhce->bhde', kdec_n, v_new)
        return S, o

    xs = tuple(jnp.moveaxis(t, 1, 0) for t in (qg, kdec, u, w, attn, g_last))
    S0 = jnp.zeros((b, GDN_HEADS, GDN_DK, GDN_DV), F32)
    _, o = lax.scan(step, S0, xs)
    o = from_chunks(jnp.moveaxis(o, 0, 1))
    o = rms_norm(o, norm_w) * jax.nn.silu(gate.astype(F32).reshape(b, T, GDN_HEADS, GDN_DV))
    return o.reshape(b, T, GDN_HEADS * GDN_DV)


def s5(u, lam_re, lam_im, b_re, b_im, c_re, c_im, d_skip, log_step, w_glu):
    bsz, T, _ = u.shape
    u = u.astype(F32).reshape(bsz, T, S5_GROUPS, S5_GROUP)
    dt = jnp.exp(log_step.astype(F32))[:, None]
    lr, li = lam_re.astype(F32), lam_im.astype(F32)
    mag = jnp.exp(lr * dt)
    ang = li * dt
    ab_re, ab_im = mag * jnp.cos(ang), mag * jnp.sin(ang)
    den = lr * lr + li * li
    nr = ab_re - 1.0
    f_re = (nr * lr + ab_im * li) / den
    f_im = (ab_im * lr - nr * li) / den
    br, bi = b_re.astype(F32), b_im.astype(F32)
    bb_re = f_re[..., None] * br - f_im[..., None] * bi
    bb_im = f_re[..., None] * bi + f_im[..., None] * br
    bu_re = jnp.einsum('btgc,gpc->btgp', u, bb_re)
    bu_im = jnp.einsum('btgc,gpc->btgp', u, bb_im)
    a_re = jnp.broadcast_to(ab_re, bu_re.shape)
    a_im = jnp.broadcast_to(ab_im, bu_im.shape)

    def combine(e1, e2):
        a1r, a1i, b1r, b1i = e1
        a2r, a2i, b2r, b2i = e2
        return (a2r * a1r - a2i * a1i, a2r * a1i + a2i * a1r,
                a2r * b1r - a2i * b1i + b2r, a2r * b1i + a2i * b1r + b2i)

    _, _, xr, xi = lax.associative_scan(combine, (a_re, a_im, bu_re, bu_im), axis=1)
    y = (jnp.einsum('btgp,gcp->btgc', xr, c_re.astype(F32))
         - jnp.einsum('btgp,gcp->btgc', xi, c_im.astype(F32))
         + u * d_skip.astype(F32).reshape(S5_GROUPS, S5_GROUP))
    y = jax.nn.gelu(y.reshape(bsz, T, HALF))
    return y * jax.nn.sigmoid(y @ w_glu.astype(F32))


def rwkv7(zc, shift_mu, w0, w2, a0, a2, g2, k_k, k_a, r_k, ln_w, ln_b):
    bsz, T, _ = zc.shape
    zc = zc.astype(F32)
    zc = zc + (token_shift(zc) - zc) * shift_mu.astype(F32)
    r, k, v, w_lr, a_lr, g_lr = split_cols(zc, RWKV_COLS)
    w = -jax.nn.softplus(-(w0.astype(F32) + jnp.tanh(w_lr) @ w2.astype(F32))) - 0.5
    decay = jnp.exp(-jnp.exp(w))
    a = jax.nn.sigmoid(a0.astype(F32) + a_lr @ a2.astype(F32))
    g = jax.nn.sigmoid(g_lr) @ g2.astype(F32)
    hs = lambda t: t.reshape(bsz, T, RWKV_HEADS, RWKV_HEAD)
    kk = l2norm(hs(k * k_k.astype(F32)))
    k = k * (1.0 + (a - 1.0) * k_a.astype(F32))
    r_h, k_h, v_h, w_h, a_h = hs(r), hs(k), hs(v), hs(decay), hs(a)
    b_h = kk * a_h

    def step(S, xs):
        r_t, w_t, k_t, v_t, kk_t, b_t = xs
        sa = jnp.einsum('bhij,bhj->bhi', S, -kk_t)
        S = S * w_t[:, :, None, :] + sa[..., None] * b_t[:, :, None, :] + v_t[..., None] * k_t[:, :, None, :]
        return S, jnp.einsum('bhij,bhj->bhi', S, r_t)

    xs = tuple(jnp.moveaxis(t, 1, 0) for t in (r_h, w_h, k_h, v_h, kk, b_h))
    S0 = jnp.zeros((bsz, RWKV_HEADS, RWKV_HEAD, RWKV_HEAD), F32)
    _, y = lax.scan(step, S0, xs)
    y = jnp.moveaxis(y, 0, 1)
    mu = jnp.mean(y, axis=-1, keepdims=True)
    var = jnp.mean(jnp.square(y - mu), axis=-1, keepdims=True)
    y = ((y - mu) * lax.rsqrt(var + RWKV_GN_EPS)).reshape(bsz, T, HALF) * ln_w.astype(F32) + ln_b.astype(F32)
    bonus = jnp.sum(r_h * k_h * r_k.astype(F32), axis=-1, keepdims=True) * v_h
    y = y + bonus.reshape(bsz, T, HALF)
    return y * g


def retention(q, k, v, gate):
    bsz, T, _ = q.shape
    q = q.astype(F32).reshape(bsz, T, RET_HEADS, RET_DK)
    k = k.astype(F32).reshape(bsz, T, RET_HEADS, RET_DK)
    v = v.astype(F32).reshape(bsz, T, RET_HEADS, RET_DV)
    pos = jnp.arange(T, dtype=F32)
    inv_freq = ROPE_BASE ** (-jnp.linspace(0.0, 1.0, RET_DK // 2, dtype=F32))
    ang = pos[:, None] * inv_freq[None, :]
    cos, sin = jnp.cos(ang)[None, :, None, :], jnp.sin(ang)[None, :, None, :]

    def rot(t):
        t1, t2 = jnp.split(t, 2, axis=-1)
        return jnp.concatenate([t1 * cos - t2 * sin, t1 * sin + t2 * cos], axis=-1)

    q, k = rot(q), rot(k) * (RET_DK ** -0.5)
    log_g = jnp.log(1.0 - 2.0 ** (-5.0 - jnp.arange(RET_HEADS, dtype=F32)))
    c = RET_CHUNK
    qc, kc, vc = to_chunks(q, c), to_chunks(k, c), to_chunks(v, c)
    idx = jnp.arange(c, dtype=F32)
    causal = jnp.tril(jnp.ones((c, c), dtype=bool))
    dmask = jnp.exp(jnp.where(causal, log_g[:, None, None] * (idx[:, None] - idx[None, :]), -jnp.inf))
    o_inner = jnp.einsum('bnhij,bnhje->bnhie', jnp.einsum('bnhid,bnhjd->bnhij', qc, kc) * dmask, vc)
    q_dec = qc * jnp.exp(log_g[:, None] * (idx + 1.0))[..., None]
    k_dec = kc * jnp.exp(log_g[:, None] * (c - 1.0 - idx))[..., None]
    chunk_decay = jnp.exp(log_g * c)[None, :, None, None]

    def step(R, xs):
        qd, kd, vv = xs
        o = jnp.einsum('bhcd,bhde->bhce', qd, R)
        R = R * chunk_decay + jnp.einsum('bhcd,bhce->bhde', kd, vv)
        return R, o

    xs = tuple(jnp.moveaxis(t, 1, 0) for t in (q_dec, k_dec, vc))
    R0 = jnp.zeros((bsz, RET_HEADS, RET_DK, RET_DV), F32)
    _, o_cross = lax.scan(step, R0, xs)
    o = from_chunks(o_inner + jnp.moveaxis(o_cross, 0, 1))
    o = o * lax.rsqrt(jnp.mean(o * o, axis=-1, keepdims=True) + EPS)
    return o.reshape(bsz, T, HALF) * jax.nn.silu(gate.astype(F32))


def conv_ffn(h, w_up, conv_w, w_down):
    z = causal_dwconv(h @ w_up, conv_w)
    gate, val = jnp.split(z, 2, axis=-1)
    return (jax.nn.silu(gate) * val) @ w_down


def setup_inputs(seed: int = 0) -> dict:
    key = jax.random.key(seed)
    ks = iter(jax.random.split(key, 48))

    def nrm(shape, scale):
        return jax.random.normal(next(ks), shape, F32) * scale

    def unif(shape, lo, hi):
        return jax.random.uniform(next(ks), shape, F32, lo, hi)

    ne, no = N_EVEN, N_ODD
    G, P = S5_GROUPS, S5_STATE
    x = nrm((BATCH, SEQ, D_MODEL), 1.0)
    norm_mix = 1.0 + nrm((DEPTH, D_MODEL), 0.02)
    norm_ffn = 1.0 + nrm((DEPTH, D_MODEL), 0.02)
    norm_final = 1.0 + nrm((D_MODEL,), 0.02)
    ev_w_in = nrm((ne, D_MODEL, EVEN_PROJ), D_MODEL ** -0.5)
    ev_w_out = nrm((ne, MIX_WIDTH, D_MODEL), MIX_WIDTH ** -0.5)
    gdn_conv_w = nrm((ne, GDN_CONV, 3 * HALF), GDN_CONV ** -0.5)
    gdn_a_log = jnp.log(unif((ne, GDN_HEADS), 1.0, 16.0))
    gdn_dt = jnp.exp(unif((ne, GDN_HEADS), math.log(1e-3), math.log(1e-1)))
    gdn_dt_bias = gdn_dt + jnp.log(-jnp.expm1(-gdn_dt))
    gdn_norm_w = 1.0 + nrm((ne, GDN_DV), 0.02)
    n = jnp.arange(P, dtype=F32)
    s5_lam_re = -0.5 * jnp.exp(nrm((ne, G, P), 0.05))
    s5_lam_im = math.pi * n + nrm((ne, G, P), 0.01)
    s5_b_re = nrm((ne, G, P, S5_GROUP), (2 * S5_GROUP) ** -0.5)
    s5_b_im = nrm((ne, G, P, S5_GROUP), (2 * S5_GROUP) ** -0.5)
    s5_c_re = nrm((ne, G, S5_GROUP, P), P ** -0.5)
    s5_c_im = nrm((ne, G, S5_GROUP, P), P ** -0.5)
    s5_d = nrm((ne, HALF), 1.0)
    s5_log_step = unif((ne, G), math.log(1e-3), math.log(1e-1))
    s5_w_glu = nrm((ne, HALF, HALF), HALF ** -0.5)
    od_w_in = nrm((no, D_MODEL, ODD_PROJ), D_MODEL ** -0.5)
    od_w_out = nrm((no, MIX_WIDTH, D_MODEL), MIX_WIDTH ** -0.5)
    rwkv_shift_mu = unif((no, RWKV_WIDTH), 0.0, 1.0)
    rwkv_w0 = jnp.linspace(-6.0, -1.0, HALF, dtype=F32)[None, :] + nrm((no, HALF), 0.1)
    rwkv_w2 = nrm((no, RWKV_DECAY_LORA, HALF), 0.1 * RWKV_DECAY_LORA ** -0.5)
    rwkv_a0 = nrm((no, HALF), 0.1)
    rwkv_a2 = nrm((no, RWKV_ICLR_LORA, HALF), 0.5 * RWKV_ICLR_LORA ** -0.5)
    rwkv_g2 = nrm((no, RWKV_GATE_LORA, HALF), RWKV_GATE_LORA ** -0.5)
    rwkv_k_k = 0.85 + nrm((no, HALF), 0.02)
    rwkv_k_a = 1.0 + nrm((no, HALF), 0.02)
    rwkv_r_k = nrm((no, RWKV_HEADS, RWKV_HEAD), 0.1)
    rwkv_ln_w = 1.0 + nrm((no, HALF), 0.02)
    rwkv_ln_b = nrm((no, HALF), 0.01)
    ffn_w_up = nrm((DEPTH, D_MODEL, 2 * D_FF), D_MODEL ** -0.5)
    ffn_conv_w = nrm((DEPTH, FFN_CONV, 2 * D_FF), FFN_CONV ** -0.5)
    ffn_w_down = nrm((DEPTH, D_FF, D_MODEL), D_FF ** -0.5)
    return {'x': x, 'norm_mix': norm_mix, 'norm_ffn': norm_ffn, 'norm_final': norm_final,
            'ev_w_in': ev_w_in, 'ev_w_out': ev_w_out, 'gdn_conv_w': gdn_conv_w, 'gdn_a_log': gdn_a_log,
            'gdn_dt_bias': gdn_dt_bias, 'gdn_norm_w': gdn_norm_w, 's5_lam_re': s5_lam_re,
            's5_lam_im': s5_lam_im, 's5_b_re': s5_b_re, 's5_b_im': s5_b_im, 's5_c_re': s5_c_re,
            's5_c_im': s5_c_im, 's5_d': s5_d, 's5_log_step': s5_log_step, 's5_w_glu': s5_w_glu,
            'od_w_in': od_w_in, 'od_w_out': od_w_out, 'rwkv_shift_mu': rwkv_shift_mu, 'rwkv_w0': rwkv_w0,
            'rwkv_w2': rwkv_w2, 'rwkv_a0': rwkv_a0, 'rwkv_a2': rwkv_a2, 'rwkv_g2': rwkv_g2,
            'rwkv_k_k': rwkv_k_k, 'rwkv_k_a': rwkv_k_a, 'rwkv_r_k': rwkv_r_k, 'rwkv_ln_w': rwkv_ln_w,
            'rwkv_ln_b': rwkv_ln_b, 'ffn_w_up': ffn_w_up, 'ffn_conv_w': ffn_conv_w, 'ffn_w_down': ffn_w_down}


def reference(x, norm_mix, norm_ffn, norm_final, ev_w_in, ev_w_out, gdn_conv_w, gdn_a_log, gdn_dt_bias,
              gdn_norm_w, s5_lam_re, s5_lam_im, s5_b_re, s5_b_im, s5_c_re, s5_c_im, s5_d, s5_log_step,
              s5_w_glu, od_w_in, od_w_out, rwkv_shift_mu, rwkv_w0, rwkv_w2, rwkv_a0, rwkv_a2, rwkv_g2,
              rwkv_k_k, rwkv_k_a, rwkv_r_k, rwkv_ln_w, rwkv_ln_b, ffn_w_up, ffn_conv_w, ffn_w_down):
    dt = x.dtype
    h = x
    for layer in range(DEPTH):
        hn = rms_norm(h, norm_mix[layer]).astype(dt)
        i = layer // 2
        if layer % 2 == 0:
            z = hn @ ev_w_in[i]
            aq, ak, av, ag, ab, aa, bu = split_cols(z, EVEN_COLS)
            ya = gated_deltanet(aq, ak, av, ag, ab, aa, gdn_conv_w[i], gdn_a_log[i], gdn_dt_bias[i], gdn_norm_w[i])
            yb = s5(bu, s5_lam_re[i], s5_lam_im[i], s5_b_re[i], s5_b_im[i], s5_c_re[i], s5_c_im[i],
                    s5_d[i], s5_log_step[i], s5_w_glu[i])
            mix = jnp.concatenate([ya, yb], axis=-1).astype(dt) @ ev_w_out[i]
        else:
            z = hn @ od_w_in[i]
            yc = rwkv7(z[..., :RWKV_WIDTH], rwkv_shift_mu[i], rwkv_w0[i], rwkv_w2[i], rwkv_a0[i], rwkv_a2[i],
                       rwkv_g2[i], rwkv_k_k[i], rwkv_k_a[i], rwkv_r_k[i], rwkv_ln_w[i], rwkv_ln_b[i])
            dq, dk, dv, dg = split_cols(z[..., RWKV_WIDTH:], RET_COLS)
            yd = retention(dq, dk, dv, dg)
            mix = jnp.concatenate([yc, yd], axis=-1).astype(dt) @ od_w_out[i]
        h = h + mix
        h = h + conv_ffn(rms_norm(h, norm_ffn[layer]).astype(dt), ffn_w_up[layer], ffn_conv_w[layer], ffn_w_down[layer])
    return rms_norm(h, norm_final).astype(dt)
```

```python
import os
import math
import numpy as np
import concourse.bass as bass
import concourse.mybir as mybir
from concourse.bass_utils import run_bass_kernel_spmd

F32 = mybir.dt.float32
BF16 = mybir.dt.bfloat16
I32 = mybir.dt.int32
AF = mybir.ActivationFunctionType
ALU = mybir.AluOpType
MULT, ADD, SUB = ALU.mult, ALU.add, ALU.subtract

T = 2048
D = 2048
C = 128
NCH = T // C
TB = 512
NTB = T // TB
DFF = 5632
NF = DFF // 128
EPS = 1e-6
TWO_PI = 2.0 * math.pi
ENGS = ("tensor", "vector", "scalar", "gpsimd", "sync")
NDMASEM = 16
EPOCH = 4000


class View:
    def __init__(self, b, ap):
        self.b = b
        self.ap = ap

    def __getitem__(self, idx):
        return View(self.b, self.ap[idx])

    def bc(self, shape):
        return View(self.b, self.ap.broadcast_to(list(shape)))

    def r3(self, c=C):
        return View(self.b, self.ap.rearrange("p (n c) -> p n c", c=c))


class Buf:
    def __init__(self, t, name):
        self.t = t
        self.name = name
        self.writer = None
        self.readers = []
        self.psum = False

    def __getitem__(self, idx):
        return View(self, self.t[idx])


def _v(x):
    return x.ap if isinstance(x, View) else x


def _b(*xs):
    return [x.b for x in xs if isinstance(x, View)]


class Prog:
    def __init__(self, nc):
        self.nc = nc
        self.cnt = {e: 0 for e in ENGS}
        self.esem = {}
        self.epoch = {e: 0 for e in ENGS}
        self.dsem = [nc.alloc_semaphore(name=f"s_dma{i}") for i in range(NDMASEM)]
        self.dcnt = [0] * NDMASEM
        self.dnext = 0
        self.seen = {e: {} for e in ENGS}
        self.ctx = []
        self.marks = []
        self.nbuf = 0
        self.psb = []
        self.psi = 0
        self.tb_ = []
        self.tbi = 0

    def sb(self, shape, dtype=F32, name=None):
        self.nbuf += 1
        name = f"{name or 'sb'}_{self.nbuf}"
        cm = self.nc.sbuf_tensor(name, list(shape), dtype)
        t = cm.__enter__()
        self.ctx.append(cm)
        return Buf(t, name)

    def ps(self, shape, dtype=F32, name=None):
        self.nbuf += 1
        name = f"{name or 'ps'}_{self.nbuf}"
        cm = self.nc.psum_tensor(name, list(shape), dtype)
        t = cm.__enter__()
        self.ctx.append(cm)
        b = Buf(t, name)
        b.psum = True
        return b

    def dram(self, shape, dtype=F32, name=None, ext=False):
        self.nbuf += 1
        if not ext:
            name = f"{name or 'scr'}_{self.nbuf}"
        t = self.nc.dram_tensor(name, list(shape), dtype, kind="ExternalOutput" if ext else "Internal").ap()
        return Buf(t, name)

    def push(self):
        self.marks.append(len(self.ctx))

    def pop(self):
        self.barrier()
        m = self.marks.pop()
        while len(self.ctx) > m:
            self.ctx.pop().__exit__(None, None, None)

    def psblk(self):
        b = self.psb[self.psi % len(self.psb)]
        self.psi += 1
        return b[:, 0:128]

    def tblk(self):
        b = self.tb_[self.tbi % len(self.tb_)]
        self.tbi += 1
        return b[:, :]

    def _sem(self, kind, sid):
        if kind == 'd':
            return self.dsem[sid]
        if sid not in self.esem:
            self.esem[sid] = self.nc.alloc_semaphore(name=f"s_{sid[0]}_{sid[1]}")
        return self.esem[sid]

    def _deps(self, eng, reads, writes):
        deps = []
        for b in reads:
            if b.writer is not None:
                deps.append(b.writer)
        for b in writes:
            if b.writer is not None:
                deps.append(b.writer)
            deps.extend(b.readers)
        waits = {}
        for (kind, sid, val) in deps:
            key = (kind, sid)
            if kind == 'e' and sid[0] == eng and eng in ("tensor", "sync"):
                continue
            if self.seen[eng].get(key, 0) >= val:
                continue
            waits[key] = max(waits.get(key, 0), val)
        eo = getattr(self.nc, eng)
        for key, val in waits.items():
            self.seen[eng][key] = val
            eo.wait_ge(self._sem(key[0], key[1]), val)

    def _mark(self, token, reads, writes):
        for b in writes:
            b.writer = token
            b.readers = []
        for b in reads:
            if b not in writes:
                b.readers.append(token)
                if len(b.readers) > 24:
                    best = {}
                    for tk in b.readers:
                        k = (tk[0], tk[1])
                        if k not in best or best[k][2] < tk[2]:
                            best[k] = tk
                    b.readers = list(best.values())

    def op(self, eng, fn, reads=(), writes=()):
        reads = list(dict.fromkeys(reads))
        writes = list(dict.fromkeys(writes))
        writes = writes + [b for b in reads if b.psum and b not in writes]
        self._deps(eng, reads, writes)
        if self.cnt[eng] >= EPOCH:
            self.epoch[eng] += 1
            self.cnt[eng] = 0
        self.cnt[eng] += 1
        sid = (eng, self.epoch[eng])
        fn(getattr(self.nc, eng)).then_inc(self._sem('e', sid), 1)
        self._mark(('e', sid, self.cnt[eng]), reads, writes)

    def dma(self, eng, out, in_):
        reads = _b(in_)
        writes = _b(out)
        i = self.dnext
        self.dnext = (self.dnext + 1) % NDMASEM
        self._deps(eng, reads, writes)
        eo = getattr(self.nc, eng)
        if self.dcnt[i] > 0 and self.seen[eng].get(('d', i), 0) < self.dcnt[i]:
            self.seen[eng][('d', i)] = self.dcnt[i]
            eo.wait_ge(self.dsem[i], self.dcnt[i])
        self.dcnt[i] += 16
        eo.dma_start(out=_v(out), in_=_v(in_)).then_inc(self.dsem[i], 16)
        token = ('d', i, self.dcnt[i])
        self._mark(token, reads, writes)
        return token

    def barrier(self):
        for e in ENGS:
            eo = getattr(self.nc, e)
            for e2 in ENGS:
                if e2 == e or self.cnt[e2] == 0:
                    continue
                sid = (e2, self.epoch[e2])
                if self.seen[e].get(('e', sid), 0) < self.cnt[e2]:
                    self.seen[e][('e', sid)] = self.cnt[e2]
                    eo.wait_ge(self._sem('e', sid), self.cnt[e2])
            for i in range(NDMASEM):
                if self.dcnt[i] > 0 and self.seen[e].get(('d', i), 0) < self.dcnt[i]:
                    self.seen[e][('d', i)] = self.dcnt[i]
                    eo.wait_ge(self.dsem[i], self.dcnt[i])

    def close(self):
        while self.ctx:
            self.ctx.pop().__exit__(None, None, None)

    def tt(self, out, a, b, op, eng="vector"):
        self.op(eng, lambda e: e.tensor_tensor(_v(out), _v(a), _v(b), op), _b(a, b), _b(out))

    def ts(self, out, a, s1, s2, op0, op1=None, eng="vector"):
        if op1 is None:
            self.op(eng, lambda e: e.tensor_scalar(_v(out), _v(a), _v(s1), None, op0), _b(a, s1), _b(out))
        else:
            self.op(eng, lambda e: e.tensor_scalar(_v(out), _v(a), _v(s1), _v(s2), op0, op1), _b(a, s1, s2), _b(out))

    def stt(self, out, a, sc, b, op0, op1):
        self.op("vector", lambda e: e.scalar_tensor_tensor(_v(out), _v(a), _v(sc), _v(b), op0, op1),
                _b(a, sc, b), _b(out))

    def act(self, out, a, func, bias=None, scale=None):
        kw = {}
        if bias is not None:
            kw["bias"] = _v(bias)
        if scale is not None:
            kw["scale"] = _v(scale)
        self.op("scalar", lambda e: e.activation(_v(out), _v(a), func, **kw), _b(a, bias, scale), _b(out))

    def cp(self, out, a, eng="vector"):
        if eng == "scalar":
            self.act(out, a, AF.Copy)
        else:
            self.op(eng, lambda e: e.tensor_copy(_v(out), _v(a)), _b(a), _b(out))

    def recip(self, out, a):
        self.op("vector", lambda e: e.reciprocal(_v(out), _v(a)), _b(a), _b(out))

    def memset(self, out, val, eng="vector"):
        self.op(eng, lambda e: e.memset(_v(out), val), (), _b(out))

    def mm(self, out, lhsT, rhs, start=True, stop=True):
        self.op("tensor", lambda e: e.matmul(_v(out), _v(lhsT), _v(rhs), start=start, stop=stop),
                _b(lhsT, rhs), _b(out))

    def tr(self, out, a, ident):
        self.op("tensor", lambda e: e.transpose(_v(out), _v(a), _v(ident)), _b(a, ident), _b(out))

    def scan(self, out, d0, d1, init, op0, op1):
        self.op("vector", lambda e: e.tensor_tensor_scan(_v(out), _v(d0), _v(d1), init, op0, op1),
                _b(d0, d1), _b(out))

    def asel(self, out, in_, pattern, cmp, fill, base, cm):
        self.op("gpsimd", lambda e: e.affine_select(_v(out), _v(in_), pattern, cmp, fill, base=base,
                                                    channel_multiplier=cm), _b(in_), _b(out))

    def iota(self, out, pattern, base, cm):
        self.op("gpsimd", lambda e: e.iota(_v(out), pattern, base=base, channel_multiplier=cm), (), _b(out))


def tbs(tb):
    return slice(tb * TB, (tb + 1) * TB)


class K:
    pass


def build_consts(P, G):
    G.IDENT = P.sb([128, 128], F32, "ident")
    P.memset(G.IDENT[:, :], 1.0, "gpsimd")
    P.asel(G.IDENT[:, :], G.IDENT[:, :], [[-1, 128]], ALU.is_equal, 0.0, 0, 1)
    G.ONES = P.sb([128, 128], F32, "ones")
    P.memset(G.ONES[:, :], 1.0, "gpsimd")
    G.CU01 = P.sb([128, 128], F32, "cu01")
    P.asel(G.CU01[:, :], G.ONES[:, :], [[1, 128]], ALU.is_ge, 0.0, 0, -1)
    G.SU01 = P.sb([128, 128], F32, "su01")
    P.asel(G.SU01[:, :], G.ONES[:, :], [[1, 128]], ALU.is_gt, 0.0, 0, -1)
    G.NEGU = P.sb([128, 128], F32, "negu")
    P.memset(G.NEGU[:, :], 0.0, "gpsimd")
    P.asel(G.NEGU[:, :], G.NEGU[:, :], [[1, 128]], ALU.is_ge, -1e30, 0, -1)
    G.SWAP = P.sb([128, 128], F32, "swap")
    P.cp(G.SWAP[:, 0:64], G.IDENT[:, 64:128])
    P.cp(G.SWAP[:, 64:128], G.IDENT[:, 0:64])
    G.CMASK = P.sb([128, T], F32, "cmask")
    P.memset(G.CMASK[:, :], 1.0, "gpsimd")
    P.memset(G.CMASK[:, :].r3()[:, :, 0:1], 0.0, "gpsimd")
    G.TPOS = P.sb([128, T], F32, "tpos")
    P.push()
    ti = P.sb([128, T], I32, "tposi")
    P.iota(ti[:, :], [[1, T]], 0, 0)
    P.cp(G.TPOS[:, :], ti[:, :])
    P.pop()
    G.CST = P.sb([128, 4], F32, "cst")
    P.memset(G.CST[:, 0:1], 1.0)
    P.memset(G.CST[:, 1:2], math.pi / 2)
    P.memset(G.CST[:, 2:4], 0.0)
    G.ONE = G.CST[:, 0:1]
    G.HPI = G.CST[:, 1:2]
    G.BK = [P.ps([128, 512], F32, f"bank{i}") for i in range(8)]
    G.PB = G.BK[0:6]
    P.psb = list(G.BK)


def sincos_from_cycles(P, G, kf, ki, kr, SN, CS):
    P.cp(ki, kf)
    P.cp(kr, ki)
    P.tt(kf, kf, kr, SUB)
    P.act(SN, kf, AF.Sin, scale=TWO_PI)
    P.stt(kr, kf, -1.0, kf, MULT, ALU.max)
    P.act(CS, kr, AF.Sin, bias=G.HPI, scale=-TWO_PI)


def rmsnorm(P, G, src, nwcol, dst_bf=None, dst_dram=None):
    P.push()
    ld = [P.sb([128, T], F32, "rn_ld") for _ in range(2)]
    sq = [P.sb([128, T], F32, "rn_sq") for _ in range(2)]
    rstd = P.sb([128, T], F32, "rn_rstd")
    for k in range(16):
        P.dma("sync", ld[k % 2][:, :], src[k])
        P.act(sq[k % 2][:, :], ld[k % 2][:, :], AF.Square)
        for tb in range(NTB):
            P.mm(G.PB[tb][:, :], G.ONES[:, :], sq[k % 2][:, tbs(tb)], start=(k == 0), stop=(k == 15))
    for tb in range(NTB):
        P.ts(rstd[:, tbs(tb)], G.PB[tb][:, :], 1.0 / D, EPS, MULT, ADD)
    P.act(rstd[:, :], rstd[:, :], AF.Sqrt)
    P.recip(rstd[:, :], rstd[:, :])
    for k in range(16):
        P.dma("sync", ld[k % 2][:, :], src[k])
        if dst_bf is not None:
            P.stt(dst_bf[:, k, :], ld[k % 2][:, :], nwcol[:, k:k + 1], rstd[:, :], MULT, MULT)
        else:
            P.stt(sq[k % 2][:, :], ld[k % 2][:, :], nwcol[:, k:k + 1], rstd[:, :], MULT, MULT)
            P.dma("sync", dst_dram[k], sq[k % 2][:, :])
    P.pop()


def proj(P, G, wt, tiles, actT, KT, evac, wbufs, nps=4):
    for idx, c in enumerate(tiles):
        wb = wbufs[idx % 2]
        P.dma("gpsimd", wb[:, 0:KT, :], wt[c])
        for tb in range(NTB):
            ps = G.PB[(idx * NTB + tb) % nps]
            for k in range(KT):
                P.mm(ps[:, :], wb[:, k, :], actT[:, k, tbs(tb)], start=(k == 0), stop=(k == KT - 1))
            evac(idx, c, tb, ps)


def in_proj(P, G, hsrc, nwcol, wt, ntiles, zscr):
    P.push()
    hn = P.sb([128, 16, T], BF16, "hn")
    rmsnorm(P, G, hsrc, nwcol, dst_bf=hn)
    wb = [P.sb([128, 16, 128], BF16, "wb") for _ in range(2)]
    zb = [P.sb([128, T], F32, "zb") for _ in range(2)]

    def evac(idx, c, tb, ps):
        P.act(zb[idx % 2][:, tbs(tb)], ps[:, :], AF.Copy)
        if tb == NTB - 1:
            P.dma("sync", zscr[c][:, :], zb[idx % 2][:, :])
    proj(P, G, wt, range(ntiles), hn, 16, evac, wb)
    P.pop()


def out_proj(P, G, yscr, wt, hsrc, hdst):
    P.push()
    yb = P.sb([128, 16, T], BF16, "yb")
    for k in range(16):
        P.dma("sync", yb[:, k, :], yscr[k][:, :])
    wb = [P.sb([128, 16, 128], BF16, "wb") for _ in range(2)]
    hb = [P.sb([128, T], F32, "hb") for _ in range(2)]

    def evac(idx, c, tb, ps):
        if tb == 0:
            P.dma("sync", hb[idx % 2][:, :], hsrc[c])
        P.tt(hb[idx % 2][:, tbs(tb)], ps[:, :], hb[idx % 2][:, tbs(tb)], ADD)
        if tb == NTB - 1:
            P.dma("sync", hdst[c], hb[idx % 2][:, :])
    proj(P, G, wt, range(16), yb, 16, evac, wb)
    P.pop()


def ffn(P, G, hsrc, hdst, nwcol, wup, cw_d, wdn, ascr):
    P.push()
    hn = P.sb([128, 16, T], BF16, "hn")
    rmsnorm(P, G, hsrc, nwcol, dst_bf=hn)
    cw = P.sb([128, 2 * NF, 3], F32, "fcw")
    P.dma("sync", cw[:, :, :], cw_d)
    wb = [P.sb([128, 16, 128], BF16, "wb") for _ in range(2)]
    zb = [P.sb([128, T], F32, "zb") for _ in range(2)]
    cb = [P.sb([128, T], F32, "cb") for _ in range(2)]
    ab = [P.sb([128, T], BF16, "ab") for _ in range(2)]
    order = []
    for f in range(NF):
        order += [f, NF + f]

    def evac(idx, c, tb, ps):
        z = zb[idx % 2]
        P.act(z[:, tbs(tb)], ps[:, :], AF.Copy)
        if tb == NTB - 1:
            cc = cb[idx % 2]
            P.ts(cc[:, :], z[:, :], cw[:, c, 2:3], None, MULT)
            P.stt(cc[:, 1:], z[:, :T - 1], cw[:, c, 1:2], cc[:, 1:], MULT, ADD)
            P.stt(cc[:, 2:], z[:, :T - 2], cw[:, c, 0:1], cc[:, 2:], MULT, ADD)
            if idx % 2 == 0:
                P.act(cc[:, :], cc[:, :], AF.Silu)
            else:
                f = c - NF
                a = ab[f % 2]
                P.tt(a[:, :], cb[0][:, :], cc[:, :], MULT)
                P.dma("sync", ascr[f][:, :], a[:, :])
    proj(P, G, wup, order, hn, 16, evac, wb)
    P.pop()
    P.push()
    HT = T // 2
    at = P.sb([128, NF, HT], BF16, "at")
    wd = [P.sb([128, NF, 128], BF16, "wd") for _ in range(2)]
    hb = [P.sb([128, HT], F32, "hb") for _ in range(2)]
    for half in range(2):
        hs = slice(half * HT, (half + 1) * HT)
        for f in range(NF):
            P.dma("sync", at[:, f, :], ascr[f][:, hs])
        for c in range(16):
            w = wd[c % 2]
            P.dma("gpsimd", w[:, :, :], wdn[c])
            h_ = hb[c % 2]
            P.dma("sync", h_[:, :], hsrc[c][:, hs])
            for tb in range(2):
                ps = G.PB[(c * 2 + tb) % 4]
                for k in range(NF):
                    P.mm(ps[:, :], w[:, k, :], at[:, k, tbs(tb)], start=(k == 0), stop=(k == NF - 1))
                P.tt(h_[:, tbs(tb)], ps[:, :], h_[:, tbs(tb)], ADD)
            P.dma("sync", hdst[c][:, hs], h_[:, :])
    P.pop()


def invert(P, G, Q0, P0):
    Y = P.tblk()
    P.tt(Y, G.IDENT[:, :], Q0, SUB)
    Pk, Qk = P0, Q0
    for lvl in range(1, 7):
        pp = P.psblk()
        P.mm(pp, Qk, Pk)
        Pn = P.tblk()
        P.act(Pn, pp, AF.Copy)
        Qn = None
        if lvl < 6:
            pq = P.psblk()
            P.mm(pq, Pk, Qk)
            Qn = P.tblk()
            P.cp(Qn, pq)
        py = P.psblk()
        P.mm(py, G.IDENT[:, :], Y, start=True, stop=False)
        P.mm(py, Pn, Y, start=False, stop=True)
        Yn = P.tblk()
        if lvl % 2:
            P.cp(Yn, py)
        else:
            P.act(Yn, py, AF.Copy)
        Y, Pk, Qk = Yn, Pn, Qn
    return Y


def alloc_tblk(P, n=64):
    P.tb_ = [P.sb([128, 128], F32, "tblk") for _ in range(n)]
    P.tbi = 0


def gdn(P, G, zscr, yscr, I):
    P.push()
    alloc_tblk(P)
    cw = P.sb([128, 24, 4], F32, "gcw")
    P.dma("sync", cw[:, :, :], I["gdn_cw"])
    ab = P.sb([128, 16], F32, "gab")
    P.dma("sync", ab[:, :], I["gdn_ab"].partition_broadcast(128))
    nw = P.sb([128, 1], F32, "gnw")
    P.dma("sync", nw[:, :], I["gdn_nw"])
    negA = P.sb([128, 8], F32, "negA")
    P.act(negA[:, :], ab[:, 0:8], AF.Exp)
    P.ts(negA[:, :], negA[:, :], -1.0, None, MULT)
    ba = P.sb([16, T], F32, "gba")
    P.dma("sync", ba[:, :], zscr[32][0:16, :])
    selb = P.sb([16, 8, 128], F32, "selb")
    sela = P.sb([16, 8, 128], F32, "sela")
    P.memset(selb[:, :, :], 1.0, "gpsimd")
    P.asel(selb[:, :, :], selb[:, :, :], [[-1, 8], [0, 128]], ALU.is_equal, 0.0, 0, 1)
    P.memset(sela[:, :, :], 1.0, "gpsimd")
    P.asel(sela[:, :, :], sela[:, :, :], [[-1, 8], [0, 128]], ALU.is_equal, 0.0, -8, 1)
    tiles = [P.sb([128, T], F32, f"g{i}") for i in range(12)]
    q, k, v, gate, tmp, BB, GC, EG, kbg, qg, kdec, O = tiles
    yb = P.sb([128, T], BF16, "gyb")
    Sst = P.sb([128, 128], F32, "gS")
    PB = G.PB
    CUT = int(os.environ.get("KCUT", "99"))
    NH = int(os.environ.get("KHEADS", "8"))
    for h in range(NH):
        P.dma("sync", q[:, :], zscr[h][:, :])
        P.dma("sync", k[:, :], zscr[8 + h][:, :])
        P.dma("sync", v[:, :], zscr[16 + h][:, :])
        P.dma("sync", gate[:, :], zscr[24 + h][:, :])
        for ti, x in enumerate((q, k, v)):
            w = cw[:, ti * 8 + h, :]
            P.ts(tmp[:, :], x[:, :], w[:, 3:4], None, MULT)
            for i in (2, 1, 0):
                sh = 3 - i
                P.stt(tmp[:, sh:], x[:, :T - sh], w[:, i:i + 1], tmp[:, sh:], MULT, ADD)
            P.act(x[:, :], tmp[:, :], AF.Silu)
        if CUT == 1:
            break
        for x, sc in ((q, 128.0 ** -0.5), (k, 1.0)):
            P.act(tmp[:, :], x[:, :], AF.Square)
            for tb in range(NTB):
                P.mm(PB[tb][:, :], G.ONES[:, :], tmp[:, tbs(tb)])
            for tb in range(NTB):
                P.ts(BB[:, tbs(tb)], PB[tb][:, :], EPS, None, ADD)
            P.act(BB[:, :], BB[:, :], AF.Sqrt)
            P.recip(BB[:, :], BB[:, :])
            P.stt(x[:, :], x[:, :], sc, BB[:, :], MULT, MULT)
        if CUT == 2:
            break
        for tb in range(NTB):
            P.mm(PB[tb][:, :], selb[:, h, :], ba[:, tbs(tb)])
            P.act(BB[:, tbs(tb)], PB[tb][:, :], AF.Sigmoid)
        for tb in range(NTB):
            P.mm(PB[tb][:, :], sela[:, h, :], ba[:, tbs(tb)])
            P.act(tmp[:, tbs(tb)], PB[tb][:, :], AF.Exp, bias=ab[:, 8 + h:9 + h])
        P.act(tmp[:, :], tmp[:, :], AF.Ln, bias=G.ONE)
        P.ts(tmp[:, :], tmp[:, :], negA[:, h:h + 1], None, MULT)
        if CUT == 3:
            break
        P.scan(GC[:, :], G.CMASK[:, :], tmp[:, :], 0.0, MULT, ADD)
        P.act(EG[:, :], GC[:, :], AF.Exp)
        P.tt(v[:, :], v[:, :], BB[:, :], MULT)
        P.tt(tmp[:, :], BB[:, :], EG[:, :], MULT)
        P.tt(kbg[:, :], k[:, :], tmp[:, :], MULT)
        P.tt(qg[:, :], q[:, :], EG[:, :], MULT)
        P.tt(tmp[:, :].r3(), GC[:, :].r3()[:, :, C - 1:C].bc([128, NCH, C]), GC[:, :].r3(), SUB)
        P.act(tmp[:, :], tmp[:, :], AF.Exp)
        P.tt(kdec[:, :], k[:, :], tmp[:, :], MULT)
        P.memset(Sst[:, :], 0.0)
        if CUT == 4:
            break
        for n in range(NCH if CUT > 10 else 1):
            sl = slice(n * C, (n + 1) * C)
            pg = P.psblk()
            P.tr(pg, GC[:, sl], G.IDENT[:, :])
            t2 = P.tblk()
            P.stt(t2, GC[:, sl], pg[:, 0:1], G.NEGU[:, :], SUB, ADD)
            DcT = P.tblk()
            P.act(DcT, t2, AF.Exp)
            pk = P.psblk()
            P.mm(pk, k[:, sl], k[:, sl])
            pq = P.psblk()
            P.mm(pq, k[:, sl], q[:, sl])
            M1 = P.tblk()
            P.tt(M1, DcT, BB[:, sl], MULT)
            P.tt(M1, M1, G.SU01[:, :], MULT)
            AT = P.tblk()
            P.tt(AT, pk, M1, MULT)
            attT = P.tblk()
            P.tt(attT, pq, DcT, MULT)
            pa = P.psblk()
            P.tr(pa, AT, G.IDENT[:, :])
            A = P.tblk()
            P.act(A, pa, AF.Copy)
            if CUT == 5:
                break
            Y = invert(P, G, AT, A)
            if CUT == 6:
                break
            pv = P.psblk()
            P.tr(pv, v[:, sl], G.IDENT[:, :])
            vbT = P.tblk()
            P.cp(vbT, pv)
            pkb = P.psblk()
            P.tr(pkb, kbg[:, sl], G.IDENT[:, :])
            kbgT = P.tblk()
            P.act(kbgT, pkb, AF.Copy)
            pkd = P.psblk()
            P.tr(pkd, kdec[:, sl], G.IDENT[:, :])
            kdT = P.tblk()
            P.cp(kdT, pkd)
            pw = P.psblk()
            P.mm(pw, kbgT, Y)
            nwT = P.tblk()
            P.act(nwT, pw, AF.Copy, scale=-1.0)
            pvn = P.psblk()
            P.mm(pvn, Y, vbT, start=True, stop=False)
            P.mm(pvn, nwT, Sst[:, :], start=False, stop=True)
            vn = P.tblk()
            P.cp(vn, pvn)
            po = P.psblk()
            P.mm(po, vn, attT, start=True, stop=False)
            P.mm(po, Sst[:, :], qg[:, sl], start=False, stop=True)
            P.act(O[:, sl], po, AF.Copy)
            psu = P.psblk()
            P.mm(psu, kdT, vn)
            P.stt(Sst[:, :], Sst[:, :], EG[:, n * C + C - 1:n * C + C], psu, MULT, ADD)
        P.act(tmp[:, :], O[:, :], AF.Square)
        for tb in range(NTB):
            P.mm(PB[tb][:, :], G.ONES[:, :], tmp[:, tbs(tb)])
        for tb in range(NTB):
            P.ts(BB[:, tbs(tb)], PB[tb][:, :], 1.0 / 128, EPS, MULT, ADD)
        P.act(BB[:, :], BB[:, :], AF.Sqrt)
        P.recip(BB[:, :], BB[:, :])
        P.stt(O[:, :], O[:, :], nw[:, 0:1], BB[:, :], MULT, MULT)
        P.act(gate[:, :], gate[:, :], AF.Silu)
        P.tt(yb[:, :], O[:, :], gate[:, :], MULT)
        P.dma("sync", yscr[h][:, :], yb[:, :])
    P.pop()


def s5(P, G, zscr, yscr, I):
    P.push()
    P.psb = G.BK[6:8]
    lam = P.sb([128, 3, 32], F32, "lam")
    P.dma("sync", lam[:, :, :], I["s5_lam"])
    dsk = P.sb([128, 8], F32, "s5d")
    P.dma("sync", dsk[:, :], I["s5_d"])
    W = [P.sb([128, 32], F32, f"s5w{i}") for i in range(12)]
    wi = P.sb([128, 32], I32, "s5wi")
    dt, RHO, PHI, sn, cs, are, aim, den, fre, fim, t1, t2 = W
    lr, li, ls = lam[:, 0, :], lam[:, 1, :], lam[:, 2, :]
    P.act(dt[:, :], ls, AF.Exp)
    P.tt(t1[:, :], lr, dt[:, :], MULT)
    P.act(RHO[:, :], t1[:, :], AF.Exp)
    P.tt(PHI[:, :], li, dt[:, :], MULT)
    P.ts(PHI[:, :], PHI[:, :], 1.0 / TWO_PI, None, MULT)
    P.cp(t1[:, :], PHI[:, :])
    sincos_from_cycles(P, G, t1[:, :], wi[:, :], t2[:, :], sn[:, :], cs[:, :])
    P.cp(PHI[:, :], t1[:, :])
    P.tt(are[:, :], RHO[:, :], cs[:, :], MULT)
    P.tt(aim[:, :], RHO[:, :], sn[:, :], MULT)
    P.tt(den[:, :], lr, lr, MULT)
    P.tt(t1[:, :], li, li, MULT)
    P.tt(den[:, :], den[:, :], t1[:, :], ADD)
    P.recip(den[:, :], den[:, :])
    P.ts(are[:, :], are[:, :], -1.0, None, ADD)
    P.tt(t1[:, :], are[:, :], lr, MULT)
    P.tt(t2[:, :], aim[:, :], li, MULT)
    P.tt(t1[:, :], t1[:, :], t2[:, :], ADD)
    P.tt(fre[:, :], t1[:, :], den[:, :], MULT)
    P.tt(t1[:, :], aim[:, :], lr, MULT)
    P.tt(t2[:, :], are[:, :], li, MULT)
    P.tt(t1[:, :], t1[:, :], t2[:, :], SUB)
    P.tt(fim[:, :], t1[:, :], den[:, :], MULT)
    zb = P.sb([128, 2, 4, 128], F32, "s5zb")
    zc = P.sb([128, 2, 4, 128], F32, "s5zc")
    bbr = P.sb([128, 4, 128], F32, "bbr")
    bbi = P.sb([128, 4, 128], F32, "bbi")
    bt1 = P.sb([128, 4, 128], F32, "bt1")
    bt2 = P.sb([128, 4, 128], F32, "bt2")
    BTr = P.sb([128, 4, 128], F32, "BTr")
    BTi = P.sb([128, 4, 128], F32, "BTi")
    u, kf, kr, SN, CS, m1, m2, m3, m4 = [P.sb([128, T], F32, f"s5t{i}") for i in range(9)]
    ki = P.sb([128, T], I32, "s5ki")
    ygb = P.sb([128, 8, T], BF16, "ygb")
    PB = G.PB
    for j in range(8):
        P.dma("sync", u[:, :], zscr[33 + j][:, :])
        P.dma("sync", zb[:, :, :, :], I["s5_zb"][:, :, 4 * j:4 * j + 4, :])
        P.dma("sync", zc[:, :, :, :], I["s5_zc"][:, :, 4 * j:4 * j + 4, :])
        P.ts(zc[:, 1, :, :], zc[:, 1, :, :], -1.0, None, MULT)
        fr_b = fre[:, 4 * j:4 * j + 4].ap.unsqueeze(2).broadcast_to([128, 4, 128])
        fi_b = fim[:, 4 * j:4 * j + 4].ap.unsqueeze(2).broadcast_to([128, 4, 128])
        fr_b = View(fre, fr_b)
        fi_b = View(fim, fi_b)
        P.tt(bt1[:, :, :], zb[:, 0, :, :], fr_b, MULT)
        P.tt(bt2[:, :, :], zb[:, 1, :, :], fi_b, MULT)
        P.tt(bbr[:, :, :], bt1[:, :, :], bt2[:, :, :], SUB)
        P.tt(bt1[:, :, :], zb[:, 1, :, :], fr_b, MULT)
        P.tt(bt2[:, :, :], zb[:, 0, :, :], fi_b, MULT)
        P.tt(bbi[:, :, :], bt1[:, :, :], bt2[:, :, :], ADD)
        for qq in range(4):
            p1 = P.psblk()
            P.tr(p1, bbr[:, qq, :], G.IDENT[:, :])
            P.cp(BTr[:, qq, :], p1)
            p2 = P.psblk()
            P.tr(p2, bbi[:, qq, :], G.IDENT[:, :])
            P.act(BTi[:, qq, :], p2, AF.Copy)
        for qq in range(4):
            q = 4 * j + qq
            P.ts(kf[:, :], G.TPOS[:, :], PHI[:, q:q + 1], None, MULT)
            sincos_from_cycles(P, G, kf[:, :], ki[:, :], kr[:, :], SN[:, :], CS[:, :])
            for tb in range(NTB):
                s_ = tbs(tb)
                P.mm(PB[4][:, :], BTr[:, qq, :], u[:, s_])
                P.mm(PB[5][:, :], BTi[:, qq, :], u[:, s_])
                P.tt(m1[:, s_], PB[4][:, :], CS[:, s_], MULT)
                P.tt(m2[:, s_], PB[5][:, :], SN[:, s_], MULT)
                P.tt(m3[:, s_], PB[5][:, :], CS[:, s_], MULT)
                P.tt(m4[:, s_], PB[4][:, :], SN[:, s_], MULT)
            P.tt(m1[:, :], m1[:, :], m2[:, :], ADD)
            P.tt(m3[:, :], m3[:, :], m4[:, :], SUB)
            rho_b = RHO[:, q:q + 1].bc([128, T])
            P.scan(m2[:, :], rho_b, m1[:, :], 0.0, MULT, ADD)
            P.scan(m4[:, :], rho_b, m3[:, :], 0.0, MULT, ADD)
            P.tt(m1[:, :], m2[:, :], CS[:, :], MULT)
            P.tt(kr[:, :], m4[:, :], SN[:, :], MULT)
            P.tt(m1[:, :], m1[:, :], kr[:, :], SUB)
            P.tt(m3[:, :], m2[:, :], SN[:, :], MULT)
            P.tt(kr[:, :], m4[:, :], CS[:, :], MULT)
            P.tt(m3[:, :], m3[:, :], kr[:, :], ADD)
            for tb in range(NTB):
                s_ = tbs(tb)
                P.mm(PB[tb][:, :], zc[:, 0, qq, :], m1[:, s_], start=(qq == 0), stop=False)
                P.mm(PB[tb][:, :], zc[:, 1, qq, :], m3[:, s_], start=False, stop=(qq == 3))
        for tb in range(NTB):
            s_ = tbs(tb)
            P.stt(kf[:, s_], u[:, s_], dsk[:, j:j + 1], PB[tb][:, :], MULT, ADD)
        P.act(ygb[:, j, :], kf[:, :], AF.Gelu_apprx_tanh)
    wb = [P.sb([128, 8, 128], BF16, "wbg") for _ in range(2)]
    yo = [P.sb([128, T], BF16, "yo") for _ in range(2)]

    def evac(idx, c, tb, ps):
        P.act(kf[:, tbs(tb)], ps[:, :], AF.Sigmoid)
        P.tt(yo[idx % 2][:, tbs(tb)], kf[:, tbs(tb)], ygb[:, c, tbs(tb)], MULT)
        if tb == NTB - 1:
            P.dma("sync", yscr[8 + c][:, :], yo[idx % 2][:, :])
    proj(P, G, I["s5_wglu"], range(8), ygb, 8, evac, wb)
    P.pop()
    P.psb = list(G.BK)


def retention(P, G, zscr, yscr, zbase):
    P.push()
    alloc_tblk(P, 32)
    pi_ = P.sb([128, 1], I32, "pidx")
    P.iota(pi_[0:64, :], [[0, 1]], 0, 1)
    P.iota(pi_[64:128, :], [[0, 1]], 0, 1)
    invf = P.sb([128, 1], F32, "invf")
    P.cp(invf[:, :], pi_[:, :])
    P.act(invf[:, :], invf[:, :], AF.Exp, scale=-math.log(10000.0) / 63.0)
    P.ts(invf[:, :], invf[:, :], 1.0 / TWO_PI, None, MULT)
    tiles = [P.sb([128, T], F32, f"r{i}") for i in range(14)]
    COS2, SIN2, qz, kz, v0, v1, g0, g1, O0, O1, qd, kd, tmp, tmp2 = tiles
    ki = P.sb([128, T], I32, "rki")
    P.ts(tmp[:, :], G.TPOS[:, :], invf[:, 0:1], None, MULT)
    sincos_from_cycles(P, G, tmp[:, :], ki[:, :], tmp2[:, :], SIN2[:, :], COS2[:, :])
    P.ts(SIN2[0:64, :], SIN2[0:64, :], -1.0, None, MULT)
    idf_i = P.sb([128, 128], I32, "idfi")
    P.iota(idf_i[:, :], [[1, 128]], 0, -1)
    idf = P.sb([128, 128], F32, "idf")
    P.cp(idf[:, :], idf_i[:, :])
    io_i = P.sb([128, 128], I32, "ioi")
    P.iota(io_i[:, :], [[1, 128]], 0, 0)
    iof = P.sb([128, 128], F32, "iof")
    P.cp(iof[:, :], io_i[:, :])
    DMT = P.sb([128, 4, 128], F32, "dmt")
    GQ = P.sb([128, 4, 128], F32, "gq")
    GK = P.sb([128, 4, 128], F32, "gk")
    lgs = [math.log(1.0 - 2.0 ** (-5.0 - h)) for h in range(4)]
    for h in range(4):
        P.act(DMT[:, h, :], idf[:, :], AF.Exp, scale=lgs[h])
        P.tt(DMT[:, h, :], DMT[:, h, :], G.CU01[:, :], MULT)
        P.ts(GQ[:, h, :], iof[:, :], lgs[h], lgs[h], MULT, ADD)
        P.act(GQ[:, h, :], GQ[:, h, :], AF.Exp)
        P.ts(GK[:, h, :], iof[:, :], -lgs[h], lgs[h] * (C - 1), MULT, ADD)
        P.act(GK[:, h, :], GK[:, h, :], AF.Exp)
    R = P.sb([128, 256], F32, "retR")
    yb = [P.sb([128, T], BF16, "ryb") for _ in range(2)]
    PB = G.PB
    for h in range(4):
        P.dma("sync", qz[:, :], zscr[zbase + h][:, :])
        P.dma("sync", kz[:, :], zscr[zbase + 4 + h][:, :])
        P.dma("sync", v0[:, :], zscr[zbase + 8 + 2 * h][:, :])
        P.dma("sync", v1[:, :], zscr[zbase + 9 + 2 * h][:, :])
        P.dma("sync", g0[:, :], zscr[zbase + 16 + 2 * h][:, :])
        P.dma("sync", g1[:, :], zscr[zbase + 17 + 2 * h][:, :])
        for x, sc in ((qz, None), (kz, 128.0 ** -0.5)):
            for tb in range(NTB):
                P.mm(PB[tb][:, :], G.SWAP[:, :], x[:, tbs(tb)])
                P.tt(tmp[:, tbs(tb)], PB[tb][:, :], SIN2[:, tbs(tb)], MULT)
            P.tt(x[:, :], x[:, :], COS2[:, :], MULT)
            P.tt(x[:, :], x[:, :], tmp[:, :], ADD)
            if sc is not None:
                P.ts(x[:, :], x[:, :], sc, None, MULT)
        gq_b = View(GQ, GQ[:, h, :].ap.unsqueeze(1).broadcast_to([128, NCH, C]))
        gk_b = View(GK, GK[:, h, :].ap.unsqueeze(1).broadcast_to([128, NCH, C]))
        P.tt(qd[:, :].r3(), qz[:, :].r3(), gq_b, MULT)
        P.tt(kd[:, :].r3(), kz[:, :].r3(), gk_b, MULT)
        P.memset(R[:, :], 0.0)
        gC = math.exp(lgs[h] * C)
        for n in range(NCH):
            sl = slice(n * C, (n + 1) * C)
            pq = P.psblk()
            P.mm(pq, kz[:, sl], qz[:, sl])
            attT = P.tblk()
            P.tt(attT, pq, DMT[:, h, :], MULT)
            ptk = P.psblk()
            P.tr(ptk, kd[:, sl], G.IDENT[:, :])
            kdT = P.tblk()
            P.cp(kdT, ptk)
            vT = []
            for e_, vv in enumerate((v0, v1)):
                pv = P.psblk()
                P.tr(pv, vv[:, sl], G.IDENT[:, :])
                vt = P.tblk()
                P.act(vt, pv, AF.Copy)
                vT.append(vt)
            for e_, OO in enumerate((O0, O1)):
                po = P.psblk()
                P.mm(po, vT[e_], attT, start=True, stop=False)
                P.mm(po, R[:, e_ * 128:(e_ + 1) * 128], qd[:, sl], start=False, stop=True)
                P.act(OO[:, sl], po, AF.Copy)
            for e_ in range(2):
                pr = P.psblk()
                P.mm(pr, kdT, vT[e_])
                P.stt(R[:, e_ * 128:(e_ + 1) * 128], R[:, e_ * 128:(e_ + 1) * 128], gC, pr, MULT, ADD)
        P.act(tmp[:, :], O0[:, :], AF.Square)
        P.act(tmp2[:, :], O1[:, :], AF.Square)
        for tb in range(NTB):
            P.mm(PB[tb][:, :], G.ONES[:, :], tmp[:, tbs(tb)], start=True, stop=False)
            P.mm(PB[tb][:, :], G.ONES[:, :], tmp2[:, tbs(tb)], start=False, stop=True)
        for tb in range(NTB):
            P.ts(qd[:, tbs(tb)], PB[tb][:, :], 1.0 / 256, EPS, MULT, ADD)
        P.act(qd[:, :], qd[:, :], AF.Sqrt)
        P.recip(qd[:, :], qd[:, :])
        for e_, (OO, gg) in enumerate(((O0, g0), (O1, g1))):
            P.tt(OO[:, :], OO[:, :], qd[:, :], MULT)
            P.act(gg[:, :], gg[:, :], AF.Silu)
            P.tt(yb[e_][:, :], OO[:, :], gg[:, :], MULT)
            P.dma("sync", yscr[8 + 2 * h + e_][:, :], yb[e_][:, :])
    P.pop()


def rwkv(P, G, zscr, yscr, I):
    P.push()
    alloc_tblk(P, 40)
    H = 64
    mu = P.sb([128, 52], F32, "mu")
    P.dma("sync", mu[:, :], I["rw_mu"])
    omu = P.sb([128, 52], F32, "omu")
    P.ts(omu[:, :], mu[:, :], -1.0, 1.0, MULT, ADD)
    vec = P.sb([64, 16, 8], F32, "rwvec")
    P.dma("sync", vec[:, :, :], I["rw_vec"])
    P.ts(vec[:, :, 7:8], vec[:, :, 3:4], -1.0, 1.0, MULT, ADD)
    w2 = P.sb([64, 1024], F32, "rw_w2")
    P.dma("sync", w2[:, :], I["rw_w2"])
    a2 = P.sb([64, 1024], F32, "rw_a2")
    P.dma("sync", a2[:, :], I["rw_a2"])
    g2 = P.sb([128, 2, 1024], F32, "rw_g2")
    P.dma("sync", g2[:, :, :], I["rw_g2"])
    ones64 = G.ONES[0:64, 0:64]
    raw = P.sb([128, T], F32, "rwraw")

    def shift_load(dst, zi, rows):
        P.dma("sync", raw[0:rows, :], zscr[zi][0:rows, :])
        P.ts(dst[0:rows, :], raw[0:rows, :], omu[0:rows, zi:zi + 1], None, MULT)
        P.stt(dst[0:rows, 1:], raw[0:rows, :T - 1], mu[0:rows, zi:zi + 1], dst[0:rows, 1:], MULT, ADD)

    TW = P.sb([64, T], F32, "rwTW")
    AL = P.sb([64, T], F32, "rwAL")
    SG1 = P.sb([128, T], F32, "rwSG1")
    SG2 = P.sb([32, T], F32, "rwSG2")
    shift_load(TW, 48, 64)
    P.act(TW[:, :], TW[:, :], AF.Tanh)
    shift_load(AL, 49, 64)
    shift_load(SG1, 50, 128)
    P.act(SG1[:, :], SG1[:, :], AF.Sigmoid)
    shift_load(SG2, 51, 32)
    P.act(SG2[:, :], SG2[:, :], AF.Sigmoid)
    S = [P.sb([64, T], F32, f"rws{i}") for i in range(11)]
    s0, s1, s2, s3, s4, s5_, s6, s7, s8, s9, O = S
    yb = P.sb([64, T], BF16, "rwyb")
    Hst = P.sb([64, 64], F32, "rwH")
    PB = G.PB
    NEG_E = -math.exp(-0.5)
    for h in range(16):
        hc = slice(h * H, (h + 1) * H)
        vv = lambda i: vec[:, h, i:i + 1]
        r, k, v = s0, s1, s2
        shift_load(r, 3 * h, 64)
        shift_load(k, 3 * h + 1, 64)
        shift_load(v, 3 * h + 2, 64)
        for tb in range(NTB):
            P.mm(PB[tb][0:64, :], a2[:, hc], AL[:, tbs(tb)])
            P.act(s4[:, tbs(tb)], PB[tb][0:64, :], AF.Sigmoid, bias=vv(1))
        P.ts(s5_[:, :], k[:, :], vv(2), None, MULT)
        P.act(s3[:, :], s5_[:, :], AF.Square)
        for tb in range(NTB):
            P.mm(PB[tb][0:64, :], ones64, s3[:, tbs(tb)])
        for tb in range(NTB):
            P.ts(s6[:, tbs(tb)], PB[tb][0:64, :], 1e-6, None, ADD)
        P.act(s6[:, :], s6[:, :], AF.Sqrt)
        P.recip(s6[:, :], s6[:, :])
        P.tt(s5_[:, :], s5_[:, :], s6[:, :], MULT)
        P.ts(s3[:, :], s4[:, :], vv(3), vv(7), MULT, ADD)
        P.tt(k[:, :], k[:, :], s3[:, :], MULT)
        P.tt(s4[:, :], s5_[:, :], s4[:, :], MULT)
        P.stt(s3[:, :], r[:, :], vv(4), k[:, :], MULT, MULT)
        for tb in range(NTB):
            P.mm(PB[tb][0:64, :], ones64, s3[:, tbs(tb)])
        for tb in range(NTB):
            P.tt(s6[:, tbs(tb)], PB[tb][0:64, :], v[:, tbs(tb)], MULT)
        for tb in range(NTB):
            P.mm(PB[tb][0:64, :], w2[:, hc], TW[:, tbs(tb)])
            P.act(s3[:, tbs(tb)], PB[tb][0:64, :], AF.Sigmoid, bias=vv(0))
        P.ts(s3[:, :], s3[:, :], NEG_E, None, MULT)
        P.scan(s7[:, :], G.CMASK[0:64, :], s3[:, :], 0.0, MULT, ADD)
        P.act(s8[:, :], s7[:, :], AF.Exp)
        P.act(s3[:, :], s7[:, :], AF.Exp, scale=-1.0)
        P.memset(s9[:, :].r3()[:, :, 0:1], 1.0)
        P.cp(s9[:, :].r3()[:, :, 1:], s8[:, :].r3()[:, :, :C - 1], "gpsimd")
        P.tt(s5_[:, :], s5_[:, :], s9[:, :], MULT)
        P.tt(s4[:, :], s4[:, :], s3[:, :], MULT)
        P.tt(k[:, :], k[:, :], s3[:, :], MULT)
        P.tt(r[:, :], r[:, :], s8[:, :], MULT)
        wc_b = s8[:, :].r3()[:, :, C - 1:C].bc([64, NCH, C])
        P.tt(s9[:, :].r3(), s4[:, :].r3(), wc_b, MULT)
        P.tt(s3[:, :].r3(), k[:, :].r3(), wc_b, MULT)
        for tb in range(NTB):
            P.mm(PB[tb][0:64, :], g2[:, 0, hc], SG1[:, tbs(tb)], start=True, stop=False)
            P.mm(PB[tb][0:64, :], g2[0:32, 1, hc], SG2[:, tbs(tb)], start=False, stop=True)
            P.act(s7[:, tbs(tb)], PB[tb][0:64, :], AF.Copy)
        kkW, bW, kW, rW, bWC, kWC = s5_, s4, k, r, s9, s3
        P.memset(Hst[:, :], 0.0)
        for n in range(NCH):
            sl = slice(n * C, (n + 1) * C)
            p1 = P.psblk()
            P.mm(p1, bW[:, sl], kkW[:, sl])
            AT = P.tblk()
            P.tt(AT, p1, G.SU01[:, :], MULT)
            p2 = P.psblk()
            P.mm(p2, bW[:, sl], rW[:, sl])
            ArT = P.tblk()
            P.tt(ArT, p2, G.CU01[:, :], MULT)
            p3 = P.psblk()
            P.mm(p3, kW[:, sl], kkW[:, sl])
            BT = P.tblk()
            P.tt(BT, p3, G.SU01[:, :], MULT)
            p4 = P.psblk()
            P.mm(p4, kW[:, sl], rW[:, sl])
            BrT = P.tblk()
            P.tt(BrT, p4, G.CU01[:, :], MULT)
            pa = P.psblk()
            P.tr(pa, AT, G.IDENT[:, :])
            A = P.tblk()
            P.act(A, pa, AF.Copy)
            Y = invert(P, G, AT, A)
            toks = []
            for src in (v, kkW, bWC, kWC):
                pt = P.psblk()
                P.tr(pt[:, 0:64], src[:, sl], G.IDENT[0:64, 0:64])
                tk = P.tblk()
                P.cp(tk[:, 0:64], pt[:, 0:64])
                toks.append(tk)
            Vt, KKWt, bWCt, kWCt = toks
            pbv = P.psblk()
            P.mm(pbv[:, 0:64], BT, Vt[:, 0:64])
            BV = P.tblk()
            P.act(BV[:, 0:64], pbv[:, 0:64], AF.Copy)
            pwp = P.psblk()
            P.mm(pwp[0:64, :], KKWt[:, 0:64], Y)
            wpT = P.tblk()
            P.act(wpT[0:64, :], pwp[0:64, :], AF.Copy)
            pu = P.psblk()
            P.mm(pu[:, 0:64], Y, BV[:, 0:64], start=True, stop=False)
            P.mm(pu[:, 0:64], wpT[0:64, :], Hst[:, :], start=False, stop=True)
            U = P.tblk()
            P.ts(U[:, 0:64], pu[:, 0:64], -1.0, None, MULT)
            po = P.psblk()
            P.mm(po[0:64, :], Hst[:, :], rW[:, sl], start=True, stop=False)
            P.mm(po[0:64, :], U[:, 0:64], ArT, start=False, stop=False)
            P.mm(po[0:64, :], Vt[:, 0:64], BrT, start=False, stop=True)
            P.act(O[:, sl], po[0:64, :], AF.Copy)
            ph = P.psblk()
            P.mm(ph[0:64, 0:64], bWCt[:, 0:64], U[:, 0:64], start=True, stop=False)
            P.mm(ph[0:64, 0:64], kWCt[:, 0:64], Vt[:, 0:64], start=False, stop=True)
            P.stt(Hst[:, :], Hst[:, :], s8[:, n * C + C - 1:n * C + C], ph[0:64, 0:64], MULT, ADD)
        for tb in range(NTB):
            P.mm(PB[tb][0:64, :], ones64, O[:, tbs(tb)])
        for tb in range(NTB):
            P.stt(s0[:, tbs(tb)], PB[tb][0:64, :], -1.0 / H, O[:, tbs(tb)], MULT, ADD)
        P.act(s1[:, :], s0[:, :], AF.Square)
        for tb in range(NTB):
            P.mm(PB[tb][0:64, :], ones64, s1[:, tbs(tb)])
        for tb in range(NTB):
            P.ts(s1[:, tbs(tb)], PB[tb][0:64, :], 1.0 / H, 64e-5, MULT, ADD)
        P.act(s1[:, :], s1[:, :], AF.Sqrt)
        P.recip(s1[:, :], s1[:, :])
        P.tt(s0[:, :], s0[:, :], s1[:, :], MULT)
        P.ts(s0[:, :], s0[:, :], vv(5), vv(6), MULT, ADD)
        P.tt(s0[:, :], s0[:, :], s6[:, :], ADD)
        P.tt(yb[:, :], s0[:, :], s7[:, :], MULT)
        P.dma("sync", yscr[h // 2][(h % 2) * 64:(h % 2) * 64 + 64, :], yb[:, :])
    P.pop()


IN_SPECS = {
    "xT": ([16, 128, T], F32),
    "nw": ([128, 5, 16], F32),
    "ev_win": ([41, 128, 16, 128], F32),
    "ev_wout": ([16, 128, 16, 128], F32),
    "gdn_cw": ([128, 24, 4], F32),
    "gdn_ab": ([16], F32),
    "gdn_nw": ([128, 1], F32),
    "s5_lam": ([128, 3, 32], F32),
    "s5_zb": ([128, 2, 32, 128], F32),
    "s5_zc": ([128, 2, 32, 128], F32),
    "s5_d": ([128, 8], F32),
    "s5_wglu": ([8, 128, 8, 128], F32),
    "od_win": ([76, 128, 16, 128], F32),
    "od_wout": ([16, 128, 16, 128], F32),
    "rw_mu": ([128, 52], F32),
    "rw_vec": ([64, 16, 8], F32),
    "rw_w2": ([64, 1024], F32),
    "rw_a2": ([64, 1024], F32),
    "rw_g2": ([128, 2, 1024], F32),
    "w_up": ([2, 88, 128, 16, 128], F32),
    "ffn_cw": ([2, 128, 88, 3], F32),
    "w_dn": ([2, 16, 128, NF, 128], F32),
}


def build_program(stage=99, mode=None):
    nc = bass.Bass("TRN2", target_bir_lowering=False)
    specs = dict(IN_SPECS)
    if mode is not None:
        need = {"G": ["gdn_cw", "gdn_ab", "gdn_nw"], "S": ["s5_lam", "s5_zb", "s5_zc", "s5_d", "s5_wglu"],
                "R": ["rw_mu", "rw_vec", "rw_w2", "rw_a2", "rw_g2"], "T": []}[mode]
        specs = {k: IN_SPECS[k] for k in need + ["nw"]}
        specs["zdbg"] = ([41 if mode in "GS" else 76, 128, T], F32)
    I = {k: nc.dram_tensor(k, list(s), d, kind="ExternalInput").ap() for k, (s, d) in specs.items()}
    outT = nc.dram_tensor("outT", [16, 128, T], F32, kind="ExternalOutput").ap()
    P = Prog(nc)
    G = K()
    build_consts(P, G)
    nw = P.sb([128, 5, 16], F32, "nw")
    P.dma("sync", nw[:, :, :], I["nw"])
    dbg = bool(int(os.environ.get("KDEBUG", "0")))
    yscr = [P.dram([128, T], BF16, f"dbg_y0_{i}", ext=dbg) for i in range(16)]
    yscr1 = [P.dram([128, T], BF16, f"dbg_y1_{i}", ext=dbg) for i in range(16)]
    outb = Buf(outT, "outT")
    out_v = [outb[k] for k in range(16)]
    if mode is not None:
        zscr = [Buf(I["zdbg"][i], f"zd{i}") for i in range(specs["zdbg"][0][0])]
        if mode == "G":
            gdn(P, G, zscr, yscr, I)
        elif mode == "S":
            s5(P, G, zscr, yscr, I)
        elif mode == "R":
            rwkv(P, G, zscr, yscr, I)
        elif mode == "T":
            retention(P, G, zscr, yscr, 52)
        P.barrier()
        P.close()
        return nc
    zscr = [P.dram([128, T], F32, f"z{i}") for i in range(76)]
    ascr = [P.dram([128, T], BF16, f"a{i}") for i in range(NF)]
    hA = [P.dram([128, T], F32, f"dbg_hA_{i}", ext=dbg) for i in range(16)]
    hB = [P.dram([128, T], F32, f"dbg_hB_{i}", ext=dbg) for i in range(16)]
    hC = [P.dram([128, T], F32, f"dbg_hC_{i}", ext=dbg) for i in range(16)]
    hD = [P.dram([128, T], F32, f"dbg_hD_{i}", ext=dbg) for i in range(16)]
    x_t = [I["xT"][k] for k in range(16)]
    hA_v = [b[:, :] for b in hA]
    hB_v = [b[:, :] for b in hB]
    hC_v = [b[:, :] for b in hC]
    hD_v = [b[:, :] for b in hD]
    in_proj(P, G, x_t, nw[:, 0, :], I["ev_win"], 41, zscr)
    gdn(P, G, zscr, yscr, I)
    s5(P, G, zscr, yscr, I)
    out_proj(P, G, yscr, I["ev_wout"], x_t, hA_v)
    ffn(P, G, hA_v, hB_v, nw[:, 1, :], I["w_up"][0], I["ffn_cw"][0], I["w_dn"][0], ascr)
    in_proj(P, G, hB_v, nw[:, 2, :], I["od_win"], 76, zscr)
    rwkv(P, G, zscr, yscr1, I)
    retention(P, G, zscr, yscr1, 52)
    out_proj(P, G, yscr1, I["od_wout"], hB_v, hC_v)
    ffn(P, G, hC_v, hD_v, nw[:, 3, :], I["w_up"][1], I["ffn_cw"][1], I["w_dn"][1], ascr)
    rmsnorm(P, G, hD_v, nw[:, 4, :], dst_dram=out_v)
    P.barrier()
    P.close()
    return nc


def _tile_w(w, col_lists, kt):
    out = np.zeros((len(col_lists), 128, kt, 128), np.float32)
    w3 = w.reshape(kt, 128, w.shape[1])
    for c, cols in enumerate(col_lists):
        cols = np.asarray(cols)
        out[c, :, :, :len(cols)] = np.transpose(w3[:, :, cols], (1, 0, 2))
    return out


def _pcol(v):
    return np.ascontiguousarray(v.reshape(-1, 128).T)


def prepare_shared(inp):
    f = lambda a: np.asarray(a, np.float32)
    S = {}
    nw = np.stack([_pcol(f(inp["norm_mix"])[0]), _pcol(f(inp["norm_ffn"])[0]), _pcol(f(inp["norm_mix"])[1]),
                   _pcol(f(inp["norm_ffn"])[1]), _pcol(f(inp["norm_final"]))], axis=1)
    S["nw"] = np.ascontiguousarray(nw)
    w = f(inp["ev_w_in"])[0]
    cl = []
    for base in (0, 1024, 2048, 3072):
        for h in range(8):
            cl.append(np.arange(base + h * 128, base + (h + 1) * 128))
    cl.append(np.arange(4096, 4112))
    for j in range(8):
        cl.append(np.arange(4112 + j * 128, 4112 + (j + 1) * 128))
    S["ev_win"] = _tile_w(w, cl, 16)
    full = [np.arange(c * 128, (c + 1) * 128) for c in range(16)]
    S["ev_wout"] = _tile_w(f(inp["ev_w_out"])[0], full, 16)
    cwt = f(inp["gdn_conv_w"])[0]
    S["gdn_cw"] = np.ascontiguousarray(np.transpose(cwt.reshape(4, 24, 128), (2, 1, 0)))
    S["gdn_ab"] = np.concatenate([f(inp["gdn_a_log"])[0], f(inp["gdn_dt_bias"])[0]])
    S["gdn_nw"] = np.ascontiguousarray(f(inp["gdn_norm_w"])[0].reshape(128, 1))
    lre, lim, lst = f(inp["s5_lam_re"])[0], f(inp["s5_lam_im"])[0], f(inp["s5_log_step"])[0]
    lam = np.zeros((128, 3, 32), np.float32)
    zb = np.zeros((128, 2, 32, 128), np.float32)
    zc = np.zeros((128, 2, 32, 128), np.float32)
    bre, bim = f(inp["s5_b_re"])[0], f(inp["s5_b_im"])[0]
    cre, cim = f(inp["s5_c_re"])[0], f(inp["s5_c_im"])[0]
    for q in range(32):
        for half in range(2):
            g = 2 * q + half
            ps = slice(half * 64, half * 64 + 64)
            lam[ps, 0, q] = lre[g]
            lam[ps, 1, q] = lim[g]
            lam[ps, 2, q] = lst[g]
            ch = 16 * (2 * (q % 4) + half)
            zb[ps, 0, q, ch:ch + 16] = bre[g]
            zb[ps, 1, q, ch:ch + 16] = bim[g]
            zc[ps, 0, q, ch:ch + 16] = cre[g].T
            zc[ps, 1, q, ch:ch + 16] = cim[g].T
    S["s5_lam"], S["s5_zb"], S["s5_zc"] = lam, zb, zc
    S["s5_d"] = _pcol(f(inp["s5_d"])[0])
    S["s5_wglu"] = _tile_w(f(inp["s5_w_glu"])[0], [np.arange(c * 128, (c + 1) * 128) for c in range(8)], 8)
    w = f(inp["od_w_in"])[0]
    cl = []
    for h in range(16):
        for base in (0, 1024, 2048):
            cl.append(np.arange(base + h * 64, base + (h + 1) * 64))
    cl.append(np.arange(3072, 3136))
    cl.append(np.arange(3136, 3200))
    cl.append(np.arange(3200, 3328))
    cl.append(np.arange(3328, 3360))
    RB = 3360
    for h in range(4):
        cl.append(np.arange(RB + h * 128, RB + (h + 1) * 128))
    for h in range(4):
        cl.append(np.arange(RB + 512 + h * 128, RB + 512 + (h + 1) * 128))
    for t_ in range(8):
        cl.append(np.arange(RB + 1024 + t_ * 128, RB + 1024 + (t_ + 1) * 128))
    for t_ in range(8):
        cl.append(np.arange(RB + 2048 + t_ * 128, RB + 2048 + (t_ + 1) * 128))
    assert len(cl) == 76
    S["od_win"] = _tile_w(w, cl, 16)
    S["od_wout"] = _tile_w(f(inp["od_w_out"])[0], full, 16)
    mu_full = f(inp["rwkv_shift_mu"])[0]
    mu = np.zeros((128, 52), np.float32)
    for i in range(52):
        mu[:len(cl[i]), i] = mu_full[cl[i]]
    S["rw_mu"] = mu
    vec = np.zeros((64, 16, 8), np.float32)
    names = ["rwkv_w0", "rwkv_a0", "rwkv_k_k", "rwkv_k_a", "rwkv_r_k", "rwkv_ln_w", "rwkv_ln_b"]
    for i, nme in enumerate(names):
        vec[:, :, i] = f(inp[nme])[0].reshape(16, 64).T
    S["rw_vec"] = vec
    S["rw_w2"] = np.ascontiguousarray(f(inp["rwkv_w2"])[0])
    S["rw_a2"] = np.ascontiguousarray(f(inp["rwkv_a2"])[0])
    g2 = np.zeros((128, 2, 1024), np.float32)
    g2f = f(inp["rwkv_g2"])[0]
    g2[:, 0, :] = g2f[0:128]
    g2[0:32, 1, :] = g2f[128:160]
    S["rw_g2"] = g2
    up = f(inp["ffn_w_up"])
    cl88 = [np.arange(c * 128, (c + 1) * 128) for c in range(88)]
    S["w_up"] = np.stack([_tile_w(up[l], cl88, 16) for l in range(2)])
    fcw = f(inp["ffn_conv_w"])
    S["ffn_cw"] = np.ascontiguousarray(np.transpose(fcw.reshape(2, 3, 88, 128), (0, 3, 2, 1)))
    dn = f(inp["ffn_w_down"])
    S["w_dn"] = np.stack([_tile_w(dn[l], full, NF) for l in range(2)])
    return S


def kernel(**inputs):
    stage = int(os.environ.get("KSTAGE", "99"))
    x = np.asarray(inputs["x"], np.float32)
    S = prepare_shared(inputs)
    nc = build_program(stage)
    in_maps = []
    for core in range(4):
        m = dict(S)
        m["xT"] = np.ascontiguousarray(x[core].T.reshape(16, 128, T))
        in_maps.append(m)
    res = run_bass_kernel_spmd(nc, in_maps, core_ids=list(range(4)))
    out = np.zeros((4, T, D), np.float32)
    for b in range(4):
        out[b] = res.results[b]["outT"].reshape(D, T).T
    return out
```

```python
import os
import math
import numpy as np
import concourse.bass as bass
import concourse.mybir as mybir
from concourse.bass_utils import run_bass_kernel_spmd

F32 = mybir.dt.float32
BF16 = mybir.dt.bfloat16
I32 = mybir.dt.int32
AF = mybir.ActivationFunctionType
ALU = mybir.AluOpType
MULT, ADD, SUB = ALU.mult, ALU.add, ALU.subtract

T = 2048
D = 2048
C = 128
NCH = T // C
TB = 512
NTB = T // TB
DFF = 5632
NF = DFF // 128
EPS = 1e-6
TWO_PI = 2.0 * math.pi
ENGS = ("tensor", "vector", "scalar", "gpsimd", "sync")
NDMASEM = 16
EPOCH = 4000


class View:
    def __init__(self, b, ap):
        self.b = b
        self.ap = ap

    def __getitem__(self, idx):
        return View(self.b, self.ap[idx])

    def bc(self, shape):
        return View(self.b, self.ap.broadcast_to(list(shape)))

    def r3(self, c=C):
        return View(self.b, self.ap.rearrange("p (n c) -> p n c", c=c))


class Buf:
    def __init__(self, t, name):
        self.t = t
        self.name = name
        self.writer = None
        self.readers = []
        self.psum = False

    def __getitem__(self, idx):
        return View(self, self.t[idx])


def _v(x):
    return x.ap if isinstance(x, View) else x


def _b(*xs):
    return [x.b for x in xs if isinstance(x, View)]


class Prog:
    def __init__(self, nc):
        self.nc = nc
        self.cnt = {e: 0 for e in ENGS}
        self.esem = {}
        self.epoch = {e: 0 for e in ENGS}
        self.dsem = [nc.alloc_semaphore(name=f"s_dma{i}") for i in range(NDMASEM)]
        self.dcnt = [0] * NDMASEM
        self.dnext = 0
        self.seen = {e: {} for e in ENGS}
        self.ctx = []
        self.marks = []
        self.nbuf = 0
        self.psb = []
        self.psi = 0
        self.tb_ = []
        self.tbi = 0

    def sb(self, shape, dtype=F32, name=None):
        self.nbuf += 1
        name = f"{name or 'sb'}_{self.nbuf}"
        cm = self.nc.sbuf_tensor(name, list(shape), dtype)
        t = cm.__enter__()
        self.ctx.append(cm)
        return Buf(t, name)

    def ps(self, shape, dtype=F32, name=None):
        self.nbuf += 1
        name = f"{name or 'ps'}_{self.nbuf}"
        cm = self.nc.psum_tensor(name, list(shape), dtype)
        t = cm.__enter__()
        self.ctx.append(cm)
        b = Buf(t, name)
        b.psum = True
        return b

    def dram(self, shape, dtype=F32, name=None, ext=False):
        self.nbuf += 1
        if not ext:
            name = f"{name or 'scr'}_{self.nbuf}"
        t = self.nc.dram_tensor(name, list(shape), dtype, kind="ExternalOutput" if ext else "Internal").ap()
        return Buf(t, name)

    def push(self):
        self.marks.append(len(self.ctx))

    def pop(self):
        self.barrier()
        m = self.marks.pop()
        while len(self.ctx) > m:
            self.ctx.pop().__exit__(None, None, None)

    def psblk(self):
        b = self.psb[self.psi % len(self.psb)]
        self.psi += 1
        return b[:, 0:128]

    def tblk(self):
        b = self.tb_[self.tbi % len(self.tb_)]
        self.tbi += 1
        return b[:, :]

    def _sem(self, kind, sid):
        if kind == 'd':
            return self.dsem[sid]
        if sid not in self.esem:
            self.esem[sid] = self.nc.alloc_semaphore(name=f"s_{sid[0]}_{sid[1]}")
        return self.esem[sid]

    def _deps(self, eng, reads, writes):
        deps = []
        for b in reads:
            if b.writer is not None:
                deps.append(b.writer)
        for b in writes:
            if b.writer is not None:
                deps.append(b.writer)
            deps.extend(b.readers)
        waits = {}
        for (kind, sid, val) in deps:
            key = (kind, sid)
            if kind == 'e' and sid[0] == eng and eng in ("tensor", "sync"):
                continue
            if self.seen[eng].get(key, 0) >= val:
                continue
            waits[key] = max(waits.get(key, 0), val)
        eo = getattr(self.nc, eng)
        for key, val in waits.items():
            self.seen[eng][key] = val
            eo.wait_ge(self._sem(key[0], key[1]), val)

    def _mark(self, token, reads, writes):
        for b in writes:
            b.writer = token
            b.readers = []
        for b in reads:
            if b not in writes:
                b.readers.append(token)
                if len(b.readers) > 24:
                    best = {}
                    for tk in b.readers:
                        k = (tk[0], tk[1])
                        if k not in best or best[k][2] < tk[2]:
                            best[k] = tk
                    b.readers = list(best.values())

    def op(self, eng, fn, reads=(), writes=()):
        reads = list(dict.fromkeys(reads))
        writes = list(dict.fromkeys(writes))
        writes = writes + [b for b in reads if b.psum and b not in writes]
        self._deps(eng, reads, writes)
        if self.cnt[eng] >= EPOCH:
            self.epoch[eng] += 1
            self.cnt[eng] = 0
        self.cnt[eng] += 1
        sid = (eng, self.epoch[eng])
        fn(getattr(self.nc, eng)).then_inc(self._sem('e', sid), 1)
        self._mark(('e', sid, self.cnt[eng]), reads, writes)

    def dma(self, eng, out, in_):
        reads = _b(in_)
        writes = _b(out)
        i = self.dnext
        self.dnext = (self.dnext + 1) % NDMASEM
        self._deps(eng, reads, writes)
        eo = getattr(self.nc, eng)
        if self.dcnt[i] > 0 and self.seen[eng].get(('d', i), 0) < self.dcnt[i]:
            self.seen[eng][('d', i)] = self.dcnt[i]
            eo.wait_ge(self.dsem[i], self.dcnt[i])
        self.dcnt[i] += 16
        eo.dma_start(out=_v(out), in_=_v(in_)).then_inc(self.dsem[i], 16)
        token = ('d', i, self.dcnt[i])
        self._mark(token, reads, writes)
        return token

    def barrier(self):
        for e in ENGS:
            eo = getattr(self.nc, e)
            for e2 in ENGS:
                if e2 == e or self.cnt[e2] == 0:
                    continue
                sid = (e2, self.epoch[e2])
                if self.seen[e].get(('e', sid), 0) < self.cnt[e2]:
                    self.seen[e][('e', sid)] = self.cnt[e2]
                    eo.wait_ge(self._sem('e', sid), self.cnt[e2])
            for i in range(NDMASEM):
                if self.dcnt[i] > 0 and self.seen[e].get(('d', i), 0) < self.dcnt[i]:
                    self.seen[e][('d', i)] = self.dcnt[i]
                    eo.wait_ge(self.dsem[i], self.dcnt[i])

    def close(self):
        while self.ctx:
            self.ctx.pop().__exit__(None, None, None)

    def tt(self, out, a, b, op, eng="vector"):
        self.op(eng, lambda e: e.tensor_tensor(_v(out), _v(a), _v(b), op), _b(a, b), _b(out))

    def ts(self, out, a, s1, s2, op0, op1=None, eng="vector"):
        if op1 is None:
            self.op(eng, lambda e: e.tensor_scalar(_v(out), _v(a), _v(s1), None, op0), _b(a, s1), _b(out))
        else:
            self.op(eng, lambda e: e.tensor_scalar(_v(out), _v(a), _v(s1), _v(s2), op0, op1), _b(a, s1, s2), _b(out))

    def stt(self, out, a, sc, b, op0, op1):
        self.op("vector", lambda e: e.scalar_tensor_tensor(_v(out), _v(a), _v(sc), _v(b), op0, op1),
                _b(a, sc, b), _b(out))

    def act(self, out, a, func, bias=None, scale=None):
        kw = {}
        if bias is not None:
            kw["bias"] = _v(bias)
        if scale is not None:
            kw["scale"] = _v(scale)
        self.op("scalar", lambda e: e.activation(_v(out), _v(a), func, **kw), _b(a, bias, scale), _b(out))

    def cp(self, out, a, eng="vector"):
        if eng == "scalar":
            self.act(out, a, AF.Copy)
        else:
            self.op(eng, lambda e: e.tensor_copy(_v(out), _v(a)), _b(a), _b(out))

    def recip(self, out, a):
        self.op("vector", lambda e: e.reciprocal(_v(out), _v(a)), _b(a), _b(out))

    def memset(self, out, val, eng="vector"):
        self.op(eng, lambda e: e.memset(_v(out), val), (), _b(out))

    def mm(self, out, lhsT, rhs, start=True, stop=True):
        self.op("tensor", lambda e: e.matmul(_v(out), _v(lhsT), _v(rhs), start=start, stop=stop),
                _b(lhsT, rhs), _b(out))

    def tr(self, out, a, ident):
        self.op("tensor", lambda e: e.transpose(_v(out), _v(a), _v(ident)), _b(a, ident), _b(out))

    def scan(self, out, d0, d1, init, op0, op1):
        self.op("vector", lambda e: e.tensor_tensor_scan(_v(out), _v(d0), _v(d1), init, op0, op1),
                _b(d0, d1), _b(out))

    def asel(self, out, in_, pattern, cmp, fill, base, cm):
        self.op("gpsimd", lambda e: e.affine_select(_v(out), _v(in_), pattern, cmp, fill, base=base,
                                                    channel_multiplier=cm), _b(in_), _b(out))

    def iota(self, out, pattern, base, cm):
        self.op("gpsimd", lambda e: e.iota(_v(out), pattern, base=base, channel_multiplier=cm), (), _b(out))


def tbs(tb):
    return slice(tb * TB, (tb + 1) * TB)


class K:
    pass


def build_consts(P, G):
    G.IDENT = P.sb([128, 128], F32, "ident")
    P.memset(G.IDENT[:, :], 1.0, "gpsimd")
    P.asel(G.IDENT[:, :], G.IDENT[:, :], [[-1, 128]], ALU.is_equal, 0.0, 0, 1)
    G.ONES = P.sb([128, 128], F32, "ones")
    P.memset(G.ONES[:, :], 1.0, "gpsimd")
    G.CU01 = P.sb([128, 128], F32, "cu01")
    P.asel(G.CU01[:, :], G.ONES[:, :], [[1, 128]], ALU.is_ge, 0.0, 0, -1)
    G.SU01 = P.sb([128, 128], F32, "su01")
    P.asel(G.SU01[:, :], G.ONES[:, :], [[1, 128]], ALU.is_gt, 0.0, 0, -1)
    G.NEGU = P.sb([128, 128], F32, "negu")
    P.memset(G.NEGU[:, :], 0.0, "gpsimd")
    P.asel(G.NEGU[:, :], G.NEGU[:, :], [[1, 128]], ALU.is_ge, -1e30, 0, -1)
    G.SWAP = P.sb([128, 128], F32, "swap")
    P.cp(G.SWAP[:, 0:64], G.IDENT[:, 64:128])
    P.cp(G.SWAP[:, 64:128], G.IDENT[:, 0:64])
    G.CMASK = P.sb([128, T], F32, "cmask")
    P.memset(G.CMASK[:, :], 1.0, "gpsimd")
    P.memset(G.CMASK[:, :].r3()[:, :, 0:1], 0.0, "gpsimd")
    G.TPOS = P.sb([128, T], F32, "tpos")
    P.push()
    ti = P.sb([128, T], I32, "tposi")
    P.iota(ti[:, :], [[1, T]], 0, 0)
    P.cp(G.TPOS[:, :], ti[:, :])
    P.pop()
    G.CST = P.sb([128, 4], F32, "cst")
    P.memset(G.CST[:, 0:1], 1.0)
    P.memset(G.CST[:, 1:2], math.pi / 2)
    P.memset(G.CST[:, 2:4], 0.0)
    G.ONE = G.CST[:, 0:1]
    G.HPI = G.CST[:, 1:2]
    G.BK = [P.ps([128, 512], F32, f"bank{i}") for i in range(8)]
    G.PB = G.BK[0:6]
    P.psb = list(G.BK)


def sincos_from_cycles(P, G, kf, ki, kr, SN, CS):
    P.cp(ki, kf)
    P.cp(kr, ki)
    P.tt(kf, kf, kr, SUB)
    P.act(SN, kf, AF.Sin, scale=TWO_PI)
    P.stt(kr, kf, -1.0, kf, MULT, ALU.max)
    P.act(CS, kr, AF.Sin, bias=G.HPI, scale=-TWO_PI)


def rmsnorm(P, G, src, nwcol, dst_bf=None, dst_dram=None):
    P.push()
    ld = [P.sb([128, T], F32, "rn_ld") for _ in range(2)]
    sq = [P.sb([128, T], F32, "rn_sq") for _ in range(2)]
    rstd = P.sb([128, T], F32, "rn_rstd")
    for k in range(16):
        P.dma("sync", ld[k % 2][:, :], src[k])
        P.act(sq[k % 2][:, :], ld[k % 2][:, :], AF.Square)
        for tb in range(NTB):
            P.mm(G.PB[tb][:, :], G.ONES[:, :], sq[k % 2][:, tbs(tb)], start=(k == 0), stop=(k == 15))
    for tb in range(NTB):
        P.ts(rstd[:, tbs(tb)], G.PB[tb][:, :], 1.0 / D, EPS, MULT, ADD)
    P.act(rstd[:, :], rstd[:, :], AF.Sqrt)
    P.recip(rstd[:, :], rstd[:, :])
    for k in range(16):
        P.dma("sync", ld[k % 2][:, :], src[k])
        if dst_bf is not None:
            P.stt(dst_bf[:, k, :], ld[k % 2][:, :], nwcol[:, k:k + 1], rstd[:, :], MULT, MULT)
        else:
            P.stt(sq[k % 2][:, :], ld[k % 2][:, :], nwcol[:, k:k + 1], rstd[:, :], MULT, MULT)
            P.dma("sync", dst_dram[k], sq[k % 2][:, :])
    P.pop()


def proj(P, G, wt, tiles, actT, KT, evac, wbufs, nps=4):
    for idx, c in enumerate(tiles):
        wb = wbufs[idx % 2]
        P.dma("gpsimd", wb[:, 0:KT, :], wt[c])
        for tb in range(NTB):
            ps = G.PB[(idx * NTB + tb) % nps]
            for k in range(KT):
                P.mm(ps[:, :], wb[:, k, :], actT[:, k, tbs(tb)], start=(k == 0), stop=(k == KT - 1))
            evac(idx, c, tb, ps)


def in_proj(P, G, hsrc, nwcol, wt, ntiles, zscr):
    P.push()
    hn = P.sb([128, 16, T], BF16, "hn")
    rmsnorm(P, G, hsrc, nwcol, dst_bf=hn)
    wb = [P.sb([128, 16, 128], BF16, "wb") for _ in range(2)]
    zb = [P.sb([128, T], F32, "zb") for _ in range(2)]

    def evac(idx, c, tb, ps):
        P.act(zb[idx % 2][:, tbs(tb)], ps[:, :], AF.Copy)
        if tb == NTB - 1:
            P.dma("sync", zscr[c][:, :], zb[idx % 2][:, :])
    proj(P, G, wt, range(ntiles), hn, 16, evac, wb)
    P.pop()


def out_proj(P, G, yscr, wt, hsrc, hdst):
    P.push()
    yb = P.sb([128, 16, T], BF16, "yb")
    for k in range(16):
        P.dma("sync", yb[:, k, :], yscr[k][:, :])
    wb = [P.sb([128, 16, 128], BF16, "wb") for _ in range(2)]
    hb = [P.sb([128, T], F32, "hb") for _ in range(2)]

    def evac(idx, c, tb, ps):
        if tb == 0:
            P.dma("sync", hb[idx % 2][:, :], hsrc[c])
        P.tt(hb[idx % 2][:, tbs(tb)], ps[:, :], hb[idx % 2][:, tbs(tb)], ADD)
        if tb == NTB - 1:
            P.dma("sync", hdst[c], hb[idx % 2][:, :])
    proj(P, G, wt, range(16), yb, 16, evac, wb)
    P.pop()


def ffn(P, G, hsrc, hdst, nwcol, wup, cw_d, wdn, ascr):
    P.push()
    hn = P.sb([128, 16, T], BF16, "hn")
    rmsnorm(P, G, hsrc, nwcol, dst_bf=hn)
    cw = P.sb([128, 2 * NF, 3], F32, "fcw")
    P.dma("sync", cw[:, :, :], cw_d)
    wb = [P.sb([128, 16, 128], BF16, "wb") for _ in range(2)]
    zb = [P.sb([128, T], F32, "zb") for _ in range(2)]
    cb = [P.sb([128, T], F32, "cb") for _ in range(2)]
    ab = [P.sb([128, T], BF16, "ab") for _ in range(2)]
    order = []
    for f in range(NF):
        order += [f, NF + f]

    def evac(idx, c, tb, ps):
        z = zb[idx % 2]
        P.act(z[:, tbs(tb)], ps[:, :], AF.Copy)
        if tb == NTB - 1:
            cc = cb[idx % 2]
            P.ts(cc[:, :], z[:, :], cw[:, c, 2:3], None, MULT)
            P.stt(cc[:, 1:], z[:, :T - 1], cw[:, c, 1:2], cc[:, 1:], MULT, ADD)
            P.stt(cc[:, 2:], z[:, :T - 2], cw[:, c, 0:1], cc[:, 2:], MULT, ADD)
            if idx % 2 == 0:
                P.act(cc[:, :], cc[:, :], AF.Silu)
            else:
                f = c - NF
                a = ab[f % 2]
                P.tt(a[:, :], cb[0][:, :], cc[:, :], MULT)
                P.dma("sync", ascr[f][:, :], a[:, :])
    proj(P, G, wup, order, hn, 16, evac, wb)
    P.pop()
    P.push()
    HT = T // 2
    at = P.sb([128, NF, HT], BF16, "at")
    wd = [P.sb([128, NF, 128], BF16, "wd") for _ in range(2)]
    hb = [P.sb([128, HT], F32, "hb") for _ in range(2)]
    for half in range(2):
        hs = slice(half * HT, (half + 1) * HT)
        for f in range(NF):
            P.dma("sync", at[:, f, :], ascr[f][:, hs])
        for c in range(16):
            w = wd[c % 2]
            P.dma("gpsimd", w[:, :, :], wdn[c])
            h_ = hb[c % 2]
            P.dma("sync", h_[:, :], hsrc[c][:, hs])
            for tb in range(2):
                ps = G.PB[(c * 2 + tb) % 4]
                for k in range(NF):
                    P.mm(ps[:, :], w[:, k, :], at[:, k, tbs(tb)], start=(k == 0), stop=(k == NF - 1))
                P.tt(h_[:, tbs(tb)], ps[:, :], h_[:, tbs(tb)], ADD)
            P.dma("sync", hdst[c][:, hs], h_[:, :])
    P.pop()


def invert(P, G, Q0, P0):
    Y = P.tblk()
    P.tt(Y, G.IDENT[:, :], Q0, SUB)
    Pk, Qk = P0, Q0
    for lvl in range(1, 7):
        pp = P.psblk()
        P.mm(pp, Qk, Pk)
        Pn = P.tblk()
        P.act(Pn, pp, AF.Copy)
        Qn = None
        if lvl < 6:
            pq = P.psblk()
            P.mm(pq, Pk, Qk)
            Qn = P.tblk()
            P.cp(Qn, pq)
        py = P.psblk()
        P.mm(py, G.IDENT[:, :], Y, start=True, stop=False)
        P.mm(py, Pn, Y, start=False, stop=True)
        Yn = P.tblk()
        if lvl % 2:
            P.cp(Yn, py)
        else:
            P.act(Yn, py, AF.Copy)
        Y, Pk, Qk = Yn, Pn, Qn
    return Y


def invert_gen(P, G, Q0, P0, final_alloc):
    Y = P.tblk()
    P.tt(Y, G.IDENT[:, :], Q0, SUB)
    Pk, Qk = P0, Q0
    for lvl in range(1, 7):
        pp = P.psblk()
        P.mm(pp, Qk, Pk)
        Pn = P.tblk()
        P.act(Pn, pp, AF.Copy)
        Qn = None
        if lvl < 6:
            pq = P.psblk()
            P.mm(pq, Pk, Qk)
            Qn = P.tblk()
            P.cp(Qn, pq)
        yield
        py = P.psblk()
        P.mm(py, G.IDENT[:, :], Y, start=True, stop=False)
        P.mm(py, Pn, Y, start=False, stop=True)
        Yn = P.tblk() if lvl < 6 else final_alloc()
        if lvl % 2:
            P.cp(Yn, py)
        else:
            P.act(Yn, py, AF.Copy)
        Y, Pk, Qk = Yn, Pn, Qn
        yield
    return Y


def interleave(gens):
    gens = list(gens)
    while gens:
        for g in list(gens):
            try:
                next(g)
            except StopIteration:
                gens.remove(g)


def alloc_tblk(P, n=64):
    P.tb_ = [P.sb([128, 128], F32, "tblk") for _ in range(n)]
    P.tbi = 0


def gdn(P, G, zscr, yscr, I):
    P.push()
    alloc_tblk(P)
    cw = P.sb([128, 24, 4], F32, "gcw")
    P.dma("sync", cw[:, :, :], I["gdn_cw"])
    ab = P.sb([128, 16], F32, "gab")
    P.dma("sync", ab[:, :], I["gdn_ab"].partition_broadcast(128))
    nw = P.sb([128, 1], F32, "gnw")
    P.dma("sync", nw[:, :], I["gdn_nw"])
    negA = P.sb([128, 8], F32, "negA")
    P.act(negA[:, :], ab[:, 0:8], AF.Exp)
    P.ts(negA[:, :], negA[:, :], -1.0, None, MULT)
    ba = P.sb([16, T], F32, "gba")
    P.dma("sync", ba[:, :], zscr[32][0:16, :])
    selb = P.sb([16, 8, 128], F32, "selb")
    sela = P.sb([16, 8, 128], F32, "sela")
    P.memset(selb[:, :, :], 1.0, "gpsimd")
    P.asel(selb[:, :, :], selb[:, :, :], [[-1, 8], [0, 128]], ALU.is_equal, 0.0, 0, 1)
    P.memset(sela[:, :, :], 1.0, "gpsimd")
    P.asel(sela[:, :, :], sela[:, :, :], [[-1, 8], [0, 128]], ALU.is_equal, 0.0, -8, 1)
    tiles = [P.sb([128, T], F32, f"g{i}") for i in range(12)]
    q, k, v, gate, tmp, BB, GC, EG, kbg, qg, kdec, O = tiles
    yb = P.sb([128, T], BF16, "gyb")
    Sst = P.sb([128, 128], F32, "gS")
    res_pool = [P.sb([128, 128], F32, "gres") for _ in range(20)]
    ri = [0]

    def rblk():
        b = res_pool[ri[0] % len(res_pool)]
        ri[0] += 1
        return b[:, :]
    PB = G.PB
    CUT = int(os.environ.get("KCUT", "99"))
    NH = int(os.environ.get("KHEADS", "8"))
    for h in range(NH):
        P.dma("sync", q[:, :], zscr[h][:, :])
        P.dma("sync", k[:, :], zscr[8 + h][:, :])
        P.dma("sync", v[:, :], zscr[16 + h][:, :])
        P.dma("sync", gate[:, :], zscr[24 + h][:, :])
        for ti, x in enumerate((q, k, v)):
            w = cw[:, ti * 8 + h, :]
            P.ts(tmp[:, :], x[:, :], w[:, 3:4], None, MULT)
            for i in (2, 1, 0):
                sh = 3 - i
                P.stt(tmp[:, sh:], x[:, :T - sh], w[:, i:i + 1], tmp[:, sh:], MULT, ADD)
            P.act(x[:, :], tmp[:, :], AF.Silu)
        if CUT == 1:
            break
        for x, sc in ((q, 128.0 ** -0.5), (k, 1.0)):
            P.act(tmp[:, :], x[:, :], AF.Square)
            for tb in range(NTB):
                P.mm(PB[tb][:, :], G.ONES[:, :], tmp[:, tbs(tb)])
            for tb in range(NTB):
                P.ts(BB[:, tbs(tb)], PB[tb][:, :], EPS, None, ADD)
            P.act(BB[:, :], BB[:, :], AF.Sqrt)
            P.recip(BB[:, :], BB[:, :])
            P.stt(x[:, :], x[:, :], sc, BB[:, :], MULT, MULT)
        if CUT == 2:
            break
        for tb in range(NTB):
            P.mm(PB[tb][:, :], selb[:, h, :], ba[:, tbs(tb)])
            P.act(BB[:, tbs(tb)], PB[tb][:, :], AF.Sigmoid)
        for tb in range(NTB):
            P.mm(PB[tb][:, :], sela[:, h, :], ba[:, tbs(tb)])
            P.act(tmp[:, tbs(tb)], PB[tb][:, :], AF.Exp, bias=ab[:, 8 + h:9 + h])
        P.act(tmp[:, :], tmp[:, :], AF.Ln, bias=G.ONE)
        P.ts(tmp[:, :], tmp[:, :], negA[:, h:h + 1], None, MULT)
        if CUT == 3:
            break
        P.scan(GC[:, :], G.CMASK[:, :], tmp[:, :], 0.0, MULT, ADD)
        P.act(EG[:, :], GC[:, :], AF.Exp)
        P.tt(v[:, :], v[:, :], BB[:, :], MULT)
        P.tt(tmp[:, :], BB[:, :], EG[:, :], MULT)
        P.tt(kbg[:, :], k[:, :], tmp[:, :], MULT)
        P.tt(qg[:, :], q[:, :], EG[:, :], MULT)
        P.tt(tmp[:, :].r3(), GC[:, :].r3()[:, :, C - 1:C].bc([128, NCH, C]), GC[:, :].r3(), SUB)
        P.act(tmp[:, :], tmp[:, :], AF.Exp)
        P.tt(kdec[:, :], k[:, :], tmp[:, :], MULT)
        P.memset(Sst[:, :], 0.0)
        if CUT == 4:
            break
        res = {}

        def pre(n):
            sl = slice(n * C, (n + 1) * C)
            pg = P.psblk()
            P.tr(pg, GC[:, sl], G.IDENT[:, :])
            t2 = P.tblk()
            P.stt(t2, GC[:, sl], pg[:, 0:1], G.NEGU[:, :], SUB, ADD)
            DcT = P.tblk()
            P.act(DcT, t2, AF.Exp)
            yield
            pk = P.psblk()
            P.mm(pk, k[:, sl], k[:, sl])
            M1 = P.tblk()
            P.tt(M1, DcT, BB[:, sl], MULT)
            P.tt(M1, M1, G.SU01[:, :], MULT)
            AT = P.tblk()
            P.tt(AT, pk, M1, MULT)
            yield
            pq = P.psblk()
            P.mm(pq, k[:, sl], q[:, sl])
            attT = rblk()
            P.tt(attT, pq, DcT, MULT)
            yield
            pa = P.psblk()
            P.tr(pa, AT, G.IDENT[:, :])
            A = P.tblk()
            P.act(A, pa, AF.Copy)
            yield
            Y = yield from invert_gen(P, G, AT, A, rblk)
            pv = P.psblk()
            P.tr(pv, v[:, sl], G.IDENT[:, :])
            vbT = rblk()
            P.cp(vbT, pv)
            yield
            pkb = P.psblk()
            P.tr(pkb, kbg[:, sl], G.IDENT[:, :])
            kbgT = P.tblk()
            P.act(kbgT, pkb, AF.Copy)
            yield
            pkd = P.psblk()
            P.tr(pkd, kdec[:, sl], G.IDENT[:, :])
            kdT = rblk()
            P.cp(kdT, pkd)
            yield
            pw = P.psblk()
            P.mm(pw, kbgT, Y)
            nwT = rblk()
            P.act(nwT, pw, AF.Copy, scale=-1.0)
            res[n] = (Y, vbT, nwT, attT, kdT)
            yield

        def seqchain(ns):
            for n in ns:
                sl = slice(n * C, (n + 1) * C)
                Y, vbT, nwT, attT, kdT = res.pop(n)
                pvn = P.psblk()
                P.mm(pvn, Y, vbT, start=True, stop=False)
                P.mm(pvn, nwT, Sst[:, :], start=False, stop=True)
                vn = P.tblk()
                P.cp(vn, pvn)
                yield
                po = P.psblk()
                P.mm(po, vn, attT, start=True, stop=False)
                P.mm(po, Sst[:, :], qg[:, sl], start=False, stop=True)
                P.act(O[:, sl], po, AF.Copy)
                psu = P.psblk()
                P.mm(psu, kdT, vn)
                yield
                P.stt(Sst[:, :], Sst[:, :], EG[:, n * C + C - 1:n * C + C], psu, MULT, ADD)
                yield

        interleave([pre(0), pre(1)])
        for n0 in range(2, NCH, 2):
            interleave([pre(n0), pre(n0 + 1), seqchain([n0 - 2, n0 - 1])])
        interleave([seqchain([NCH - 2, NCH - 1])])
        P.act(tmp[:, :], O[:, :], AF.Square)
        for tb in range(NTB):
            P.mm(PB[tb][:, :], G.ONES[:, :], tmp[:, tbs(tb)])
        for tb in range(NTB):
            P.ts(BB[:, tbs(tb)], PB[tb][:, :], 1.0 / 128, EPS, MULT, ADD)
        P.act(BB[:, :], BB[:, :], AF.Sqrt)
        P.recip(BB[:, :], BB[:, :])
        P.stt(O[:, :], O[:, :], nw[:, 0:1], BB[:, :], MULT, MULT)
        P.act(gate[:, :], gate[:, :], AF.Silu)
        P.tt(yb[:, :], O[:, :], gate[:, :], MULT)
        P.dma("sync", yscr[h][:, :], yb[:, :])
    P.pop()


def s5(P, G, zscr, yscr, I):
    P.push()
    P.psb = G.BK[6:8]
    lam = P.sb([128, 3, 32], F32, "lam")
    P.dma("sync", lam[:, :, :], I["s5_lam"])
    dsk = P.sb([128, 8], F32, "s5d")
    P.dma("sync", dsk[:, :], I["s5_d"])
    W = [P.sb([128, 32], F32, f"s5w{i}") for i in range(12)]
    wi = P.sb([128, 32], I32, "s5wi")
    dt, RHO, PHI, sn, cs, are, aim, den, fre, fim, t1, t2 = W
    lr, li, ls = lam[:, 0, :], lam[:, 1, :], lam[:, 2, :]
    P.act(dt[:, :], ls, AF.Exp)
    P.tt(t1[:, :], lr, dt[:, :], MULT)
    P.act(RHO[:, :], t1[:, :], AF.Exp)
    P.tt(PHI[:, :], li, dt[:, :], MULT)
    P.ts(PHI[:, :], PHI[:, :], 1.0 / TWO_PI, None, MULT)
    P.cp(t1[:, :], PHI[:, :])
    sincos_from_cycles(P, G, t1[:, :], wi[:, :], t2[:, :], sn[:, :], cs[:, :])
    P.cp(PHI[:, :], t1[:, :])
    P.tt(are[:, :], RHO[:, :], cs[:, :], MULT)
    P.tt(aim[:, :], RHO[:, :], sn[:, :], MULT)
    P.tt(den[:, :], lr, lr, MULT)
    P.tt(t1[:, :], li, li, MULT)
    P.tt(den[:, :], den[:, :], t1[:, :], ADD)
    P.recip(den[:, :], den[:, :])
    P.ts(are[:, :], are[:, :], -1.0, None, ADD)
    P.tt(t1[:, :], are[:, :], lr, MULT)
    P.tt(t2[:, :], aim[:, :], li, MULT)
    P.tt(t1[:, :], t1[:, :], t2[:, :], ADD)
    P.tt(fre[:, :], t1[:, :], den[:, :], MULT)
    P.tt(t1[:, :], aim[:, :], lr, MULT)
    P.tt(t2[:, :], are[:, :], li, MULT)
    P.tt(t1[:, :], t1[:, :], t2[:, :], SUB)
    P.tt(fim[:, :], t1[:, :], den[:, :], MULT)
    zb = P.sb([128, 2, 4, 128], F32, "s5zb")
    zc = P.sb([128, 2, 4, 128], F32, "s5zc")
    bbr = P.sb([128, 4, 128], F32, "bbr")
    bbi = P.sb([128, 4, 128], F32, "bbi")
    bt1 = P.sb([128, 4, 128], F32, "bt1")
    bt2 = P.sb([128, 4, 128], F32, "bt2")
    BTr = P.sb([128, 4, 128], F32, "BTr")
    BTi = P.sb([128, 4, 128], F32, "BTi")
    u, kf, kr, SN, CS, m1, m2, m3, m4 = [P.sb([128, T], F32, f"s5t{i}") for i in range(9)]
    ki = P.sb([128, T], I32, "s5ki")
    ygb = P.sb([128, 8, T], BF16, "ygb")
    PB = G.PB
    for j in range(8):
        P.dma("sync", u[:, :], zscr[33 + j][:, :])
        P.dma("sync", zb[:, :, :, :], I["s5_zb"][:, :, 4 * j:4 * j + 4, :])
        P.dma("sync", zc[:, :, :, :], I["s5_zc"][:, :, 4 * j:4 * j + 4, :])
        P.ts(zc[:, 1, :, :], zc[:, 1, :, :], -1.0, None, MULT)
        fr_b = fre[:, 4 * j:4 * j + 4].ap.unsqueeze(2).broadcast_to([128, 4, 128])
        fi_b = fim[:, 4 * j:4 * j + 4].ap.unsqueeze(2).broadcast_to([128, 4, 128])
        fr_b = View(fre, fr_b)
        fi_b = View(fim, fi_b)
        P.tt(bt1[:, :, :], zb[:, 0, :, :], fr_b, MULT)
        P.tt(bt2[:, :, :], zb[:, 1, :, :], fi_b, MULT)
        P.tt(bbr[:, :, :], bt1[:, :, :], bt2[:, :, :], SUB)
        P.tt(bt1[:, :, :], zb[:, 1, :, :], fr_b, MULT)
        P.tt(bt2[:, :, :], zb[:, 0, :, :], fi_b, MULT)
        P.tt(bbi[:, :, :], bt1[:, :, :], bt2[:, :, :], ADD)
        for qq in range(4):
            p1 = P.psblk()
            P.tr(p1, bbr[:, qq, :], G.IDENT[:, :])
            P.cp(BTr[:, qq, :], p1)
            p2 = P.psblk()
            P.tr(p2, bbi[:, qq, :], G.IDENT[:, :])
            P.act(BTi[:, qq, :], p2, AF.Copy)
        for qq in range(4):
            q = 4 * j + qq
            P.ts(kf[:, :], G.TPOS[:, :], PHI[:, q:q + 1], None, MULT)
            sincos_from_cycles(P, G, kf[:, :], ki[:, :], kr[:, :], SN[:, :], CS[:, :])
            for tb in range(NTB):
                s_ = tbs(tb)
                P.mm(PB[4][:, :], BTr[:, qq, :], u[:, s_])
                P.mm(PB[5][:, :], BTi[:, qq, :], u[:, s_])
                P.tt(m1[:, s_], PB[4][:, :], CS[:, s_], MULT)
                P.tt(m2[:, s_], PB[5][:, :], SN[:, s_], MULT)
                P.tt(m3[:, s_], PB[5][:, :], CS[:, s_], MULT)
                P.tt(m4[:, s_], PB[4][:, :], SN[:, s_], MULT)
            P.tt(m1[:, :], m1[:, :], m2[:, :], ADD)
            P.tt(m3[:, :], m3[:, :], m4[:, :], SUB)
            rho_b = RHO[:, q:q + 1].bc([128, T])
            P.scan(m2[:, :], rho_b, m1[:, :], 0.0, MULT, ADD)
            P.scan(m4[:, :], rho_b, m3[:, :], 0.0, MULT, ADD)
            P.tt(m1[:, :], m2[:, :], CS[:, :], MULT)
            P.tt(kr[:, :], m4[:, :], SN[:, :], MULT)
            P.tt(m1[:, :], m1[:, :], kr[:, :], SUB)
            P.tt(m3[:, :], m2[:, :], SN[:, :], MULT)
            P.tt(kr[:, :], m4[:, :], CS[:, :], MULT)
            P.tt(m3[:, :], m3[:, :], kr[:, :], ADD)
            for tb in range(NTB):
                s_ = tbs(tb)
                P.mm(PB[tb][:, :], zc[:, 0, qq, :], m1[:, s_], start=(qq == 0), stop=False)
                P.mm(PB[tb][:, :], zc[:, 1, qq, :], m3[:, s_], start=False, stop=(qq == 3))
        for tb in range(NTB):
            s_ = tbs(tb)
            P.stt(kf[:, s_], u[:, s_], dsk[:, j:j + 1], PB[tb][:, :], MULT, ADD)
        P.act(ygb[:, j, :], kf[:, :], AF.Gelu_apprx_tanh)
    wb = [P.sb([128, 8, 128], BF16, "wbg") for _ in range(2)]
    yo = [P.sb([128, T], BF16, "yo") for _ in range(2)]

    def evac(idx, c, tb, ps):
        P.act(kf[:, tbs(tb)], ps[:, :], AF.Sigmoid)
        P.tt(yo[idx % 2][:, tbs(tb)], kf[:, tbs(tb)], ygb[:, c, tbs(tb)], MULT)
        if tb == NTB - 1:
            P.dma("sync", yscr[8 + c][:, :], yo[idx % 2][:, :])
    proj(P, G, I["s5_wglu"], range(8), ygb, 8, evac, wb)
    P.pop()
    P.psb = list(G.BK)


def retention(P, G, zscr, yscr, zbase):
    P.push()
    alloc_tblk(P, 32)
    pi_ = P.sb([128, 1], I32, "pidx")
    P.iota(pi_[0:64, :], [[0, 1]], 0, 1)
    P.iota(pi_[64:128, :], [[0, 1]], 0, 1)
    invf = P.sb([128, 1], F32, "invf")
    P.cp(invf[:, :], pi_[:, :])
    P.act(invf[:, :], invf[:, :], AF.Exp, scale=-math.log(10000.0) / 63.0)
    P.ts(invf[:, :], invf[:, :], 1.0 / TWO_PI, None, MULT)
    tiles = [P.sb([128, T], F32, f"r{i}") for i in range(14)]
    COS2, SIN2, qz, kz, v0, v1, g0, g1, O0, O1, qd, kd, tmp, tmp2 = tiles
    ki = P.sb([128, T], I32, "rki")
    P.ts(tmp[:, :], G.TPOS[:, :], invf[:, 0:1], None, MULT)
    sincos_from_cycles(P, G, tmp[:, :], ki[:, :], tmp2[:, :], SIN2[:, :], COS2[:, :])
    P.ts(SIN2[0:64, :], SIN2[0:64, :], -1.0, None, MULT)
    idf_i = P.sb([128, 128], I32, "idfi")
    P.iota(idf_i[:, :], [[1, 128]], 0, -1)
    idf = P.sb([128, 128], F32, "idf")
    P.cp(idf[:, :], idf_i[:, :])
    io_i = P.sb([128, 128], I32, "ioi")
    P.iota(io_i[:, :], [[1, 128]], 0, 0)
    iof = P.sb([128, 128], F32, "iof")
    P.cp(iof[:, :], io_i[:, :])
    DMT = P.sb([128, 4, 128], F32, "dmt")
    GQ = P.sb([128, 4, 128], F32, "gq")
    GK = P.sb([128, 4, 128], F32, "gk")
    lgs = [math.log(1.0 - 2.0 ** (-5.0 - h)) for h in range(4)]
    for h in range(4):
        P.act(DMT[:, h, :], idf[:, :], AF.Exp, scale=lgs[h])
        P.tt(DMT[:, h, :], DMT[:, h, :], G.CU01[:, :], MULT)
        P.ts(GQ[:, h, :], iof[:, :], lgs[h], lgs[h], MULT, ADD)
        P.act(GQ[:, h, :], GQ[:, h, :], AF.Exp)
        P.ts(GK[:, h, :], iof[:, :], -lgs[h], lgs[h] * (C - 1), MULT, ADD)
        P.act(GK[:, h, :], GK[:, h, :], AF.Exp)
    R = P.sb([128, 256], F32, "retR")
    yb = [P.sb([128, T], BF16, "ryb") for _ in range(2)]
    PB = G.PB
    for h in range(4):
        P.dma("sync", qz[:, :], zscr[zbase + h][:, :])
        P.dma("sync", kz[:, :], zscr[zbase + 4 + h][:, :])
        P.dma("sync", v0[:, :], zscr[zbase + 8 + 2 * h][:, :])
        P.dma("sync", v1[:, :], zscr[zbase + 9 + 2 * h][:, :])
        P.dma("sync", g0[:, :], zscr[zbase + 16 + 2 * h][:, :])
        P.dma("sync", g1[:, :], zscr[zbase + 17 + 2 * h][:, :])
        for x, sc in ((qz, None), (kz, 128.0 ** -0.5)):
            for tb in range(NTB):
                P.mm(PB[tb][:, :], G.SWAP[:, :], x[:, tbs(tb)])
                P.tt(tmp[:, tbs(tb)], PB[tb][:, :], SIN2[:, tbs(tb)], MULT)
            P.tt(x[:, :], x[:, :], COS2[:, :], MULT)
            P.tt(x[:, :], x[:, :], tmp[:, :], ADD)
            if sc is not None:
                P.ts(x[:, :], x[:, :], sc, None, MULT)
        gq_b = View(GQ, GQ[:, h, :].ap.unsqueeze(1).broadcast_to([128, NCH, C]))
        gk_b = View(GK, GK[:, h, :].ap.unsqueeze(1).broadcast_to([128, NCH, C]))
        P.tt(qd[:, :].r3(), qz[:, :].r3(), gq_b, MULT)
        P.tt(kd[:, :].r3(), kz[:, :].r3(), gk_b, MULT)
        P.memset(R[:, :], 0.0)
        gC = math.exp(lgs[h] * C)
        for n in range(NCH):
            sl = slice(n * C, (n + 1) * C)
            pq = P.psblk()
            P.mm(pq, kz[:, sl], qz[:, sl])
            attT = P.tblk()
            P.tt(attT, pq, DMT[:, h, :], MULT)
            ptk = P.psblk()
            P.tr(ptk, kd[:, sl], G.IDENT[:, :])
            kdT = P.tblk()
            P.cp(kdT, ptk)
            vT = []
            for e_, vv in enumerate((v0, v1)):
                pv = P.psblk()
                P.tr(pv, vv[:, sl], G.IDENT[:, :])
                vt = P.tblk()
                P.act(vt, pv, AF.Copy)
                vT.append(vt)
            for e_, OO in enumerate((O0, O1)):
                po = P.psblk()
                P.mm(po, vT[e_], attT, start=True, stop=False)
                P.mm(po, R[:, e_ * 128:(e_ + 1) * 128], qd[:, sl], start=False, stop=True)
                P.act(OO[:, sl], po, AF.Copy)
            for e_ in range(2):
                pr = P.psblk()
                P.mm(pr, kdT, vT[e_])
                P.stt(R[:, e_ * 128:(e_ + 1) * 128], R[:, e_ * 128:(e_ + 1) * 128], gC, pr, MULT, ADD)
        P.act(tmp[:, :], O0[:, :], AF.Square)
        P.act(tmp2[:, :], O1[:, :], AF.Square)
        for tb in range(NTB):
            P.mm(PB[tb][:, :], G.ONES[:, :], tmp[:, tbs(tb)], start=True, stop=False)
            P.mm(PB[tb][:, :], G.ONES[:, :], tmp2[:, tbs(tb)], start=False, stop=True)
        for tb in range(NTB):
            P.ts(qd[:, tbs(tb)], PB[tb][:, :], 1.0 / 256, EPS, MULT, ADD)
        P.act(qd[:, :], qd[:, :], AF.Sqrt)
        P.recip(qd[:, :], qd[:, :])
        for e_, (OO, gg) in enumerate(((O0, g0), (O1, g1))):
            P.tt(OO[:, :], OO[:, :], qd[:, :], MULT)
            P.act(gg[:, :], gg[:, :], AF.Silu)
            P.tt(yb[e_][:, :], OO[:, :], gg[:, :], MULT)
            P.dma("sync", yscr[8 + 2 * h + e_][:, :], yb[e_][:, :])
    P.pop()


def rwkv(P, G, zscr, yscr, I):
    P.push()
    alloc_tblk(P, 40)
    H = 64
    mu = P.sb([128, 52], F32, "mu")
    P.dma("sync", mu[:, :], I["rw_mu"])
    omu = P.sb([128, 52], F32, "omu")
    P.ts(omu[:, :], mu[:, :], -1.0, 1.0, MULT, ADD)
    vec = P.sb([64, 16, 8], F32, "rwvec")
    P.dma("sync", vec[:, :, :], I["rw_vec"])
    P.ts(vec[:, :, 7:8], vec[:, :, 3:4], -1.0, 1.0, MULT, ADD)
    w2 = P.sb([64, 1024], F32, "rw_w2")
    P.dma("sync", w2[:, :], I["rw_w2"])
    a2 = P.sb([64, 1024], F32, "rw_a2")
    P.dma("sync", a2[:, :], I["rw_a2"])
    g2 = P.sb([128, 2, 1024], F32, "rw_g2")
    P.dma("sync", g2[:, :, :], I["rw_g2"])
    ones64 = G.ONES[0:64, 0:64]
    res_pool = [P.sb([128, 128], F32, "rres") for _ in range(16)]
    ri = [0]

    def rblk():
        b = res_pool[ri[0] % len(res_pool)]
        ri[0] += 1
        return b[:, :]

    def shift_load(dst, zi, rows):
        P.dma("sync", raw[0:rows, :], zscr[zi][0:rows, :])
        P.ts(dst[0:rows, :], raw[0:rows, :], omu[0:rows, zi:zi + 1], None, MULT)
        P.stt(dst[0:rows, 1:], raw[0:rows, :T - 1], mu[0:rows, zi:zi + 1], dst[0:rows, 1:], MULT, ADD)

    TW = P.sb([64, T], F32, "rwTW")
    AL = P.sb([64, T], F32, "rwAL")
    SG1 = P.sb([128, T], F32, "rwSG1")
    SG2 = P.sb([32, T], F32, "rwSG2")
    raw = P.sb([128, T], F32, "rwO")
    shift_load(TW, 48, 64)
    P.act(TW[:, :], TW[:, :], AF.Tanh)
    shift_load(AL, 49, 64)
    shift_load(SG1, 50, 128)
    P.act(SG1[:, :], SG1[:, :], AF.Sigmoid)
    shift_load(SG2, 51, 32)
    P.act(SG2[:, :], SG2[:, :], AF.Sigmoid)
    S = [P.sb([64, T], F32, f"rws{i}") for i in range(10)]
    s0, s1, s2, s3, s4, s5_, s6, s7, s8, s9 = S
    O = raw
    yb = P.sb([64, T], BF16, "rwyb")
    Hst = P.sb([64, 64], F32, "rwH")
    PB = G.PB
    NEG_E = -math.exp(-0.5)
    for h in range(16):
        hc = slice(h * H, (h + 1) * H)
        vv = lambda i: vec[:, h, i:i + 1]
        r, k, v = s0, s1, s2
        shift_load(r, 3 * h, 64)
        shift_load(k, 3 * h + 1, 64)
        shift_load(v, 3 * h + 2, 64)
        for tb in range(NTB):
            P.mm(PB[tb][0:64, :], a2[:, hc], AL[:, tbs(tb)])
            P.act(s4[:, tbs(tb)], PB[tb][0:64, :], AF.Sigmoid, bias=vv(1))
        P.ts(s5_[:, :], k[:, :], vv(2), None, MULT)
        P.act(s3[:, :], s5_[:, :], AF.Square)
        for tb in range(NTB):
            P.mm(PB[tb][0:64, :], ones64, s3[:, tbs(tb)])
        for tb in range(NTB):
            P.ts(s6[:, tbs(tb)], PB[tb][0:64, :], 1e-6, None, ADD)
        P.act(s6[:, :], s6[:, :], AF.Sqrt)
        P.recip(s6[:, :], s6[:, :])
        P.tt(s5_[:, :], s5_[:, :], s6[:, :], MULT)
        P.ts(s3[:, :], s4[:, :], vv(3), vv(7), MULT, ADD)
        P.tt(k[:, :], k[:, :], s3[:, :], MULT)
        P.tt(s4[:, :], s5_[:, :], s4[:, :], MULT)
        P.stt(s3[:, :], r[:, :], vv(4), k[:, :], MULT, MULT)
        for tb in range(NTB):
            P.mm(PB[tb][0:64, :], ones64, s3[:, tbs(tb)])
        for tb in range(NTB):
            P.tt(s6[:, tbs(tb)], PB[tb][0:64, :], v[:, tbs(tb)], MULT)
        for tb in range(NTB):
            P.mm(PB[tb][0:64, :], w2[:, hc], TW[:, tbs(tb)])
            P.act(s3[:, tbs(tb)], PB[tb][0:64, :], AF.Sigmoid, bias=vv(0))
        P.ts(s3[:, :], s3[:, :], NEG_E, None, MULT)
        P.scan(s7[:, :], G.CMASK[0:64, :], s3[:, :], 0.0, MULT, ADD)
        P.act(s8[:, :], s7[:, :], AF.Exp)
        P.act(s3[:, :], s7[:, :], AF.Exp, scale=-1.0)
        P.memset(s9[:, :].r3()[:, :, 0:1], 1.0)
        P.cp(s9[:, :].r3()[:, :, 1:], s8[:, :].r3()[:, :, :C - 1], "gpsimd")
        P.tt(s5_[:, :], s5_[:, :], s9[:, :], MULT)
        P.tt(s4[:, :], s4[:, :], s3[:, :], MULT)
        P.tt(k[:, :], k[:, :], s3[:, :], MULT)
        P.tt(r[:, :], r[:, :], s8[:, :], MULT)
        wc_b = s8[:, :].r3()[:, :, C - 1:C].bc([64, NCH, C])
        P.tt(s9[:, :].r3(), s4[:, :].r3(), wc_b, MULT)
        P.tt(s3[:, :].r3(), k[:, :].r3(), wc_b, MULT)
        for tb in range(NTB):
            P.mm(PB[tb][0:64, :], g2[:, 0, hc], SG1[:, tbs(tb)], start=True, stop=False)
            P.mm(PB[tb][0:64, :], g2[0:32, 1, hc], SG2[:, tbs(tb)], start=False, stop=True)
            P.act(s7[:, tbs(tb)], PB[tb][0:64, :], AF.Copy)
        kkW, bW, kW, rW, bWC, kWC = s5_, s4, k, r, s9, s3
        P.memset(Hst[:, :], 0.0)
        res = {}

        def pre(n):
            sl = slice(n * C, (n + 1) * C)
            p1 = P.psblk()
            P.mm(p1, bW[:, sl], kkW[:, sl])
            AT = P.tblk()
            P.tt(AT, p1, G.SU01[:, :], MULT)
            yield
            p2 = P.psblk()
            P.mm(p2, bW[:, sl], rW[:, sl])
            ArT = rblk()
            P.tt(ArT, p2, G.CU01[:, :], MULT)
            yield
            p3 = P.psblk()
            P.mm(p3, kW[:, sl], kkW[:, sl])
            BT = P.tblk()
            P.tt(BT, p3, G.SU01[:, :], MULT)
            yield
            p4 = P.psblk()
            P.mm(p4, kW[:, sl], rW[:, sl])
            BrT = rblk()
            P.tt(BrT, p4, G.CU01[:, :], MULT)
            yield
            pa = P.psblk()
            P.tr(pa, AT, G.IDENT[:, :])
            A = P.tblk()
            P.act(A, pa, AF.Copy)
            yield
            Y = yield from invert_gen(P, G, AT, A, rblk)
            toks = []
            for src in (v, kkW, bWC, kWC):
                pt = P.psblk()
                P.tr(pt[:, 0:64], src[:, sl], G.IDENT[0:64, 0:64])
                tk = rblk() if src is not kkW else P.tblk()
                P.cp(tk[:, 0:64], pt[:, 0:64])
                toks.append(tk)
                yield
            Vt, KKWt, bWCt, kWCt = toks
            pbv = P.psblk()
            P.mm(pbv[:, 0:64], BT, Vt[:, 0:64])
            BV = rblk()
            P.act(BV[:, 0:64], pbv[:, 0:64], AF.Copy)
            yield
            pwp = P.psblk()
            P.mm(pwp[0:64, :], KKWt[:, 0:64], Y)
            wpT = rblk()
            P.act(wpT[0:64, :], pwp[0:64, :], AF.Copy)
            res[n] = (Y, BV, wpT, ArT, BrT, Vt, bWCt, kWCt)
            yield

        def seqchain(ns):
            for n in ns:
                sl = slice(n * C, (n + 1) * C)
                Y, BV, wpT, ArT, BrT, Vt, bWCt, kWCt = res.pop(n)
                pu = P.psblk()
                P.mm(pu[:, 0:64], Y, BV[:, 0:64], start=True, stop=False)
                P.mm(pu[:, 0:64], wpT[0:64, :], Hst[:, :], start=False, stop=True)
                U = P.tblk()
                P.ts(U[:, 0:64], pu[:, 0:64], -1.0, None, MULT)
                yield
                po = P.psblk()
                P.mm(po[0:64, :], Hst[:, :], rW[:, sl], start=True, stop=False)
                P.mm(po[0:64, :], U[:, 0:64], ArT, start=False, stop=False)
                P.mm(po[0:64, :], Vt[:, 0:64], BrT, start=False, stop=True)
                P.act(O[0:64, sl], po[0:64, :], AF.Copy)
                ph = P.psblk()
                P.mm(ph[0:64, 0:64], bWCt[:, 0:64], U[:, 0:64], start=True, stop=False)
                P.mm(ph[0:64, 0:64], kWCt[:, 0:64], Vt[:, 0:64], start=False, stop=True)
                yield
                P.stt(Hst[:, :], Hst[:, :], s8[:, n * C + C - 1:n * C + C], ph[0:64, 0:64], MULT, ADD)
                yield

        interleave([pre(0)])
        for n in range(1, NCH):
            interleave([pre(n), seqchain([n - 1])])
        interleave([seqchain([NCH - 1])])
        for tb in range(NTB):
            P.mm(PB[tb][0:64, :], ones64, O[0:64, tbs(tb)])
        for tb in range(NTB):
            P.stt(s0[:, tbs(tb)], PB[tb][0:64, :], -1.0 / H, O[0:64, tbs(tb)], MULT, ADD)
        P.act(s1[:, :], s0[:, :], AF.Square)
        for tb in range(NTB):
            P.mm(PB[tb][0:64, :], ones64, s1[:, tbs(tb)])
        for tb in range(NTB):
            P.ts(s1[:, tbs(tb)], PB[tb][0:64, :], 1.0 / H, 64e-5, MULT, ADD)
        P.act(s1[:, :], s1[:, :], AF.Sqrt)
        P.recip(s1[:, :], s1[:, :])
        P.tt(s0[:, :], s0[:, :], s1[:, :], MULT)
        P.ts(s0[:, :], s0[:, :], vv(5), vv(6), MULT, ADD)
        P.tt(s0[:, :], s0[:, :], s6[:, :], ADD)
        P.tt(yb[:, :], s0[:, :], s7[:, :], MULT)
        P.dma("sync", yscr[h // 2][(h % 2) * 64:(h % 2) * 64 + 64, :], yb[:, :])
    P.pop()


IN_SPECS = {
    "xT": ([16, 128, T], F32),
    "nw": ([128, 5, 16], F32),
    "ev_win": ([41, 128, 16, 128], F32),
    "ev_wout": ([16, 128, 16, 128], F32),
    "gdn_cw": ([128, 24, 4], F32),
    "gdn_ab": ([16], F32),
    "gdn_nw": ([128, 1], F32),
    "s5_lam": ([128, 3, 32], F32),
    "s5_zb": ([128, 2, 32, 128], F32),
    "s5_zc": ([128, 2, 32, 128], F32),
    "s5_d": ([128, 8], F32),
    "s5_wglu": ([8, 128, 8, 128], F32),
    "od_win": ([76, 128, 16, 128], F32),
    "od_wout": ([16, 128, 16, 128], F32),
    "rw_mu": ([128, 52], F32),
    "rw_vec": ([64, 16, 8], F32),
    "rw_w2": ([64, 1024], F32),
    "rw_a2": ([64, 1024], F32),
    "rw_g2": ([128, 2, 1024], F32),
    "w_up": ([2, 88, 128, 16, 128], F32),
    "ffn_cw": ([2, 128, 88, 3], F32),
    "w_dn": ([2, 16, 128, NF, 128], F32),
}


def build_program(stage=99, mode=None):
    nc = bass.Bass("TRN2", target_bir_lowering=False)
    specs = dict(IN_SPECS)
    if mode is not None:
        need = {"G": ["gdn_cw", "gdn_ab", "gdn_nw"], "S": ["s5_lam", "s5_zb", "s5_zc", "s5_d", "s5_wglu"],
                "R": ["rw_mu", "rw_vec", "rw_w2", "rw_a2", "rw_g2"], "T": [], "P": ["xT", "ev_win"]}[mode]
        specs = {k: IN_SPECS[k] for k in need + ["nw"]}
        specs["zdbg"] = ([41 if mode in "GSP" else 76, 128, T], F32)
    I = {k: nc.dram_tensor(k, list(s), d, kind="ExternalInput").ap() for k, (s, d) in specs.items()}
    outT = nc.dram_tensor("outT", [16, 128, T], F32, kind="ExternalOutput").ap()
    P = Prog(nc)
    G = K()
    build_consts(P, G)
    nw = P.sb([128, 5, 16], F32, "nw")
    P.dma("sync", nw[:, :, :], I["nw"])
    dbg = bool(int(os.environ.get("KDEBUG", "0")))
    yscr = [P.dram([128, T], BF16, f"dbg_y0_{i}", ext=dbg) for i in range(16)]
    yscr1 = [P.dram([128, T], BF16, f"dbg_y1_{i}", ext=dbg) for i in range(16)]
    outb = Buf(outT, "outT")
    out_v = [outb[k] for k in range(16)]
    if mode is not None:
        zscr = [Buf(I["zdbg"][i], f"zd{i}") for i in range(specs["zdbg"][0][0])]
        if mode == "P":
            zs2 = [P.dram([128, T], F32, f"z{i}") for i in range(41)]
            in_proj(P, G, [I["xT"][k] for k in range(16)], nw[:, 0, :], I["ev_win"], 41, zs2)
        elif mode == "G":
            gdn(P, G, zscr, yscr, I)
        elif mode == "S":
            s5(P, G, zscr, yscr, I)
        elif mode == "R":
            rwkv(P, G, zscr, yscr, I)
        elif mode == "T":
            retention(P, G, zscr, yscr, 52)
        P.barrier()
        P.close()
        return nc
    zscr = [P.dram([128, T], F32, f"z{i}") for i in range(76)]
    ascr = [P.dram([128, T], BF16, f"a{i}") for i in range(NF)]
    hA = [P.dram([128, T], F32, f"dbg_hA_{i}", ext=dbg) for i in range(16)]
    hB = [P.dram([128, T], F32, f"dbg_hB_{i}", ext=dbg) for i in range(16)]
    hC = [P.dram([128, T], F32, f"dbg_hC_{i}", ext=dbg) for i in range(16)]
    hD = [P.dram([128, T], F32, f"dbg_hD_{i}", ext=dbg) for i in range(16)]
    x_t = [I["xT"][k] for k in range(16)]
    hA_v = [b[:, :] for b in hA]
    hB_v = [b[:, :] for b in hB]
    hC_v = [b[:, :] for b in hC]
    hD_v = [b[:, :] for b in hD]
    in_proj(P, G, x_t, nw[:, 0, :], I["ev_win"], 41, zscr)
    gdn(P, G, zscr, yscr, I)
    s5(P, G, zscr, yscr, I)
    out_proj(P, G, yscr, I["ev_wout"], x_t, hA_v)
    ffn(P, G, hA_v, hB_v, nw[:, 1, :], I["w_up"][0], I["ffn_cw"][0], I["w_dn"][0], ascr)
    in_proj(P, G, hB_v, nw[:, 2, :], I["od_win"], 76, zscr)
    rwkv(P, G, zscr, yscr1, I)
    retention(P, G, zscr, yscr1, 52)
    out_proj(P, G, yscr1, I["od_wout"], hB_v, hC_v)
    ffn(P, G, hC_v, hD_v, nw[:, 3, :], I["w_up"][1], I["ffn_cw"][1], I["w_dn"][1], ascr)
    rmsnorm(P, G, hD_v, nw[:, 4, :], dst_dram=out_v)
    P.barrier()
    P.close()
    return nc


def _tile_w(w, col_lists, kt):
    out = np.zeros((len(col_lists), 128, kt, 128), np.float32)
    w3 = w.reshape(kt, 128, w.shape[1])
    for c, cols in enumerate(col_lists):
        cols = np.asarray(cols)
        out[c, :, :, :len(cols)] = np.transpose(w3[:, :, cols], (1, 0, 2))
    return out


def _pcol(v):
    return np.ascontiguousarray(v.reshape(-1, 128).T)


def prepare_shared(inp):
    f = lambda a: np.asarray(a, np.float32)
    S = {}
    nw = np.stack([_pcol(f(inp["norm_mix"])[0]), _pcol(f(inp["norm_ffn"])[0]), _pcol(f(inp["norm_mix"])[1]),
                   _pcol(f(inp["norm_ffn"])[1]), _pcol(f(inp["norm_final"]))], axis=1)
    S["nw"] = np.ascontiguousarray(nw)
    w = f(inp["ev_w_in"])[0]
    cl = []
    for base in (0, 1024, 2048, 3072):
        for h in range(8):
            cl.append(np.arange(base + h * 128, base + (h + 1) * 128))
    cl.append(np.arange(4096, 4112))
    for j in range(8):
        cl.append(np.arange(4112 + j * 128, 4112 + (j + 1) * 128))
    S["ev_win"] = _tile_w(w, cl, 16)
    full = [np.arange(c * 128, (c + 1) * 128) for c in range(16)]
    S["ev_wout"] = _tile_w(f(inp["ev_w_out"])[0], full, 16)
    cwt = f(inp["gdn_conv_w"])[0]
    S["gdn_cw"] = np.ascontiguousarray(np.transpose(cwt.reshape(4, 24, 128), (2, 1, 0)))
    S["gdn_ab"] = np.concatenate([f(inp["gdn_a_log"])[0], f(inp["gdn_dt_bias"])[0]])
    S["gdn_nw"] = np.ascontiguousarray(f(inp["gdn_norm_w"])[0].reshape(128, 1))
    lre, lim, lst = f(inp["s5_lam_re"])[0], f(inp["s5_lam_im"])[0], f(inp["s5_log_step"])[0]
    lam = np.zeros((128, 3, 32), np.float32)
    zb = np.zeros((128, 2, 32, 128), np.float32)
    zc = np.zeros((128, 2, 32, 128), np.float32)
    bre, bim = f(inp["s5_b_re"])[0], f(inp["s5_b_im"])[0]
    cre, cim = f(inp["s5_c_re"])[0], f(inp["s5_c_im"])[0]
    for q in range(32):
        for half in range(2):
            g = 2 * q + half
            ps = slice(half * 64, half * 64 + 64)
            lam[ps, 0, q] = lre[g]
            lam[ps, 1, q] = lim[g]
            lam[ps, 2, q] = lst[g]
            ch = 16 * (2 * (q % 4) + half)
            zb[ps, 0, q, ch:ch + 16] = bre[g]
            zb[ps, 1, q, ch:ch + 16] = bim[g]
            zc[ps, 0, q, ch:ch + 16] = cre[g].T
            zc[ps, 1, q, ch:ch + 16] = cim[g].T
    S["s5_lam"], S["s5_zb"], S["s5_zc"] = lam, zb, zc
    S["s5_d"] = _pcol(f(inp["s5_d"])[0])
    S["s5_wglu"] = _tile_w(f(inp["s5_w_glu"])[0], [np.arange(c * 128, (c + 1) * 128) for c in range(8)], 8)
    w = f(inp["od_w_in"])[0]
    cl = []
    for h in range(16):
        for base in (0, 1024, 2048):
            cl.append(np.arange(base + h * 64, base + (h + 1) * 64))
    cl.append(np.arange(3072, 3136))
    cl.append(np.arange(3136, 3200))
    cl.append(np.arange(3200, 3328))
    cl.append(np.arange(3328, 3360))
    RB = 3360
    for h in range(4):
        cl.append(np.arange(RB + h * 128, RB + (h + 1) * 128))
    for h in range(4):
        cl.append(np.arange(RB + 512 + h * 128, RB + 512 + (h + 1) * 128))
    for t_ in range(8):
        cl.append(np.arange(RB + 1024 + t_ * 128, RB + 1024 + (t_ + 1) * 128))
    for t_ in range(8):
        cl.append(np.arange(RB + 2048 + t_ * 128, RB + 2048 + (t_ + 1) * 128))
    assert len(cl) == 76
    S["od_win"] = _tile_w(w, cl, 16)
    S["od_wout"] = _tile_w(f(inp["od_w_out"])[0], full, 16)
    mu_full = f(inp["rwkv_shift_mu"])[0]
    mu = np.zeros((128, 52), np.float32)
    for i in range(52):
        mu[:len(cl[i]), i] = mu_full[cl[i]]
    S["rw_mu"] = mu
    vec = np.zeros((64, 16, 8), np.float32)
    names = ["rwkv_w0", "rwkv_a0", "rwkv_k_k", "rwkv_k_a", "rwkv_r_k", "rwkv_ln_w", "rwkv_ln_b"]
    for i, nme in enumerate(names):
        vec[:, :, i] = f(inp[nme])[0].reshape(16, 64).T
    S["rw_vec"] = vec
    S["rw_w2"] = np.ascontiguousarray(f(inp["rwkv_w2"])[0])
    S["rw_a2"] = np.ascontiguousarray(f(inp["rwkv_a2"])[0])
    g2 = np.zeros((128, 2, 1024), np.float32)
    g2f = f(inp["rwkv_g2"])[0]
    g2[:, 0, :] = g2f[0:128]
    g2[0:32, 1, :] = g2f[128:160]
    S["rw_g2"] = g2
    up = f(inp["ffn_w_up"])
    cl88 = [np.arange(c * 128, (c + 1) * 128) for c in range(88)]
    S["w_up"] = np.stack([_tile_w(up[l], cl88, 16) for l in range(2)])
    fcw = f(inp["ffn_conv_w"])
    S["ffn_cw"] = np.ascontiguousarray(np.transpose(fcw.reshape(2, 3, 88, 128), (0, 3, 2, 1)))
    dn = f(inp["ffn_w_down"])
    S["w_dn"] = np.stack([_tile_w(dn[l], full, NF) for l in range(2)])
    return S


def kernel(**inputs):
    stage = int(os.environ.get("KSTAGE", "99"))
    x = np.asarray(inputs["x"], np.float32)
    S = prepare_shared(inputs)
    nc = build_program(stage)
    in_maps = []
    for core in range(4):
        m = dict(S)
        m["xT"] = np.ascontiguousarray(x[core].T.reshape(16, 128, T))
        in_maps.append(m)
    res = run_bass_kernel_spmd(nc, in_maps, core_ids=list(range(4)))
    out = np.zeros((4, T, D), np.float32)
    for b in range(4):
        out[b] = res.results[b]["outT"].reshape(D, T).T
    return out
```
